# Optimizing a Trainium2 kernel written in Bass

```python
import jax, jax.numpy as jnp
from jax import lax
import numpy as np

D_MODEL = 4096
BATCH = 1
SEQ = 8192
DEPTH = 1
DEC_BATCH = 8
DEC_SEQ = 2048
PAST_LEN = 128

LRU_WIDTH = D_MODEL
LRU_HEADS = 16
LRU_BLOCK = LRU_WIDTH // LRU_HEADS
CONV_WIDTH = 4
CONV_LEFT = 2
LRU_C = 8.0
MLA_HEADS = 32
Q_LORA = 1024
KV_LORA = 512
QK_NOPE = 128
QK_ROPE = 64
V_HEAD = 128
QK_HEAD = QK_NOPE + QK_ROPE
MLA_WIDTH = MLA_HEADS * V_HEAD
ROPE_THETA = 10000.0
Q_BLOCK = 128
D_FF = 4 * D_MODEL
N_BRANCH = 2
EPS = 1e-6
IN_COLS = 2 * LRU_WIDTH + Q_LORA + KV_LORA + QK_ROPE + N_BRANCH * D_MODEL
IN_SPLITS = (LRU_WIDTH,
             2 * LRU_WIDTH,
             2 * LRU_WIDTH + Q_LORA,
             2 * LRU_WIDTH + Q_LORA + KV_LORA,
             2 * LRU_WIDTH + Q_LORA + KV_LORA + QK_ROPE)

kernel_name = "hybrid_rglru_mla_encoder"


def rmsnorm(x, g):
    xf = x.astype(jnp.float32)
    y = xf * lax.rsqrt(jnp.mean(jnp.square(xf), axis=-1, keepdims=True) + EPS)
    return (y * g.astype(jnp.float32)).astype(x.dtype)


def rope_tables(seq):
    inv = 1.0 / (ROPE_THETA ** (jnp.arange(0, QK_ROPE, 2, dtype=jnp.float32) / QK_ROPE))
    ang = jnp.arange(seq, dtype=jnp.float32)[:, None] * inv[None, :]
    return jnp.cos(ang), jnp.sin(ang)


def apply_rope(x, cos, sin):
    xf = x.astype(jnp.float32)
    x1, x2 = xf[..., :QK_ROPE // 2], xf[..., QK_ROPE // 2:]
    return jnp.concatenate([x1 * cos - x2 * sin, x1 * sin + x2 * cos], axis=-1).astype(x.dtype)


def centred_depthwise_conv(x, w, b):
    s = x.shape[1]
    xp = jnp.pad(x, ((0, 0), (CONV_LEFT, CONV_WIDTH - 1 - CONV_LEFT), (0, 0)))
    out = b
    for k in range(CONV_WIDTH):
        out = out + xp[:, k:k + s] * w[k]
    return out


def block_diag_linear(x, w, b):
    xb = x.reshape(x.shape[:-1] + (LRU_HEADS, LRU_BLOCK))
    y = jnp.einsum('bshi,hij->bshj', xb, w) + b
    return y.reshape(x.shape)


def rglru_scan(x, w_a, b_a, w_x, b_x, lam, reverse):
    r = jax.nn.sigmoid(block_diag_linear(x, w_a, b_a).astype(jnp.float32))
    i = jax.nn.sigmoid(block_diag_linear(x, w_x, b_x).astype(jnp.float32))
    log_a = -LRU_C * r * jax.nn.softplus(-lam.astype(jnp.float32))
    a = jnp.exp(log_a)
    u = jnp.sqrt(-jnp.expm1(2.0 * log_a)) * (i * x.astype(jnp.float32))

    def combine(left, right):
        a1, b1 = left
        a2, b2 = right
        return a1 * a2, a2 * b1 + b2

    _, h = lax.associative_scan(combine, (a, u), reverse=reverse, axis=1)
    return h


def mla_attention(q_nope, q_pe, k_nope, k_pe, v):
    b, s, h, _ = q_nope.shape
    nb = s // Q_BLOCK
    scale = QK_HEAD ** -0.5

    def to_blocks(t):
        return jnp.moveaxis(t.reshape((b, nb, Q_BLOCK) + t.shape[2:]), 1, 0)

    def one_block(args):
        qn, qp = args
        sc = (jnp.einsum('bqhd,bkhd->bhqk', qn, k_nope, preferred_element_type=jnp.float32)
              + jnp.einsum('bqhr,bkr->bhqk', qp, k_pe, preferred_element_type=jnp.float32)) * scale
        p = jax.nn.softmax(sc, axis=-1).astype(v.dtype)
        return jnp.einsum('bhqk,bkhd->bqhd', p, v)

    o = lax.map(one_block, (to_blocks(q_nope), to_blocks(q_pe)))
    return jnp.moveaxis(o, 0, 1).reshape(b, s, h * V_HEAD)


def hybrid_mixer(xn, w_in, conv_w, conv_b, lru_wa, lru_ba, lru_wx, lru_bx, lru_lam,
                 q_norm, w_q_up, kv_norm, w_kv_up, w_lru_proj, w_mla_proj, w_out):
    b, s, _ = xn.shape
    dt = xn.dtype
    z = xn @ w_in
    x_lru, y_lru, c_q, c_kv, k_rope, gate_logits = jnp.split(z, IN_SPLITS, axis=-1)

    xc = centred_depthwise_conv(x_lru, conv_w, conv_b)
    h = (rglru_scan(xc, lru_wa[0], lru_ba[0], lru_wx[0], lru_bx[0], lru_lam[0], False)
         + rglru_scan(xc, lru_wa[1], lru_ba[1], lru_wx[1], lru_bx[1], lru_lam[1], True))
    o_lru = (h.astype(dt) * jax.nn.gelu(y_lru)) @ w_lru_proj

    cos, sin = rope_tables(s)
    q = (rmsnorm(c_q, q_norm) @ w_q_up).reshape(b, s, MLA_HEADS, QK_HEAD)
    q_nope = q[..., :QK_NOPE]
    q_pe = apply_rope(q[..., QK_NOPE:], cos[None, :, None, :], sin[None, :, None, :])
    kv = (rmsnorm(c_kv, kv_norm) @ w_kv_up).reshape(b, s, MLA_HEADS, QK_NOPE + V_HEAD)
    k_nope = kv[..., :QK_NOPE]
    v = kv[..., QK_NOPE:]
    k_pe = apply_rope(k_rope, cos[None], sin[None])
    o_mla = mla_attention(q_nope, q_pe, k_nope, k_pe, v) @ w_mla_proj

    g = jax.nn.sigmoid(gate_logits.astype(jnp.float32))
    g_a, g_b = g[..., :D_MODEL], g[..., D_MODEL:]
    merged = (g_a * o_lru.astype(jnp.float32) + g_b * o_mla.astype(jnp.float32)).astype(dt)
    return merged @ w_out


def trunk(x, norm1, w_in, conv_w, conv_b, lru_wa, lru_ba, lru_wx, lru_bx, lru_lam,
          q_norm, w_q_up, kv_norm, w_kv_up, w_lru_proj, w_mla_proj, w_out,
          norm2, w_up, w_down, norm_f):
    for l in range(DEPTH):
        x = x + hybrid_mixer(rmsnorm(x, norm1[l]), w_in[l], conv_w[l], conv_b[l],
                             lru_wa[l], lru_ba[l], lru_wx[l], lru_bx[l], lru_lam[l],
                             q_norm[l], w_q_up[l], kv_norm[l], w_kv_up[l],
                             w_lru_proj[l], w_mla_proj[l], w_out[l])
        u = jnp.square(jax.nn.relu(rmsnorm(x, norm2[l]) @ w_up[l]))
        x = x + u @ w_down[l]
    return rmsnorm(x, norm_f)


def setup_inputs(seed: int = 0) -> dict:
    key = jax.random.key(seed)
    ks = jax.random.split(key, 24)
    f32 = jnp.float32

    def nrm(k, shape, fan_in):
        return jax.random.normal(k, shape, f32) * (fan_in ** -0.5)

    def gain(k, shape):
        return 1.0 + 0.01 * jax.random.normal(k, shape, f32)

    def bias(k, shape):
        return 0.01 * jax.random.normal(k, shape, f32)

    u = jax.random.uniform(ks[10], (DEPTH, 2, LRU_WIDTH), f32, 0.9, 0.999)
    s = u ** (1.0 / LRU_C)
    lru_lam = jnp.log(s) - jnp.log1p(-s)

    return {
        "x_prompt": jax.random.normal(ks[0], (BATCH, SEQ, D_MODEL), f32),
        "x_sample": jax.random.normal(ks[1], (DEC_BATCH, DEC_SEQ, D_MODEL), f32),
        "norm1": gain(ks[2], (DEPTH, D_MODEL)),
        "w_in": nrm(ks[3], (DEPTH, D_MODEL, IN_COLS), D_MODEL),
        "conv_w": nrm(ks[4], (DEPTH, CONV_WIDTH, LRU_WIDTH), CONV_WIDTH),
        "conv_b": bias(ks[5], (DEPTH, LRU_WIDTH)),
        "lru_wa": nrm(ks[6], (DEPTH, 2, LRU_HEADS, LRU_BLOCK, LRU_BLOCK), LRU_BLOCK),
        "lru_ba": bias(ks[7], (DEPTH, 2, LRU_HEADS, LRU_BLOCK)),
        "lru_wx": nrm(ks[8], (DEPTH, 2, LRU_HEADS, LRU_BLOCK, LRU_BLOCK), LRU_BLOCK),
        "lru_bx": bias(ks[9], (DEPTH, 2, LRU_HEADS, LRU_BLOCK)),
        "lru_lam": lru_lam,
        "q_norm": gain(ks[11], (DEPTH, Q_LORA)),
        "w_q_up": nrm(ks[12], (DEPTH, Q_LORA, MLA_HEADS * QK_HEAD), Q_LORA),
        "kv_norm": gain(ks[13], (DEPTH, KV_LORA)),
        "w_kv_up": nrm(ks[14], (DEPTH, KV_LORA, MLA_HEADS * (QK_NOPE + V_HEAD)), KV_LORA),
        "w_lru_proj": nrm(ks[15], (DEPTH, LRU_WIDTH, D_MODEL), LRU_WIDTH),
        "w_mla_proj": nrm(ks[16], (DEPTH, MLA_WIDTH, D_MODEL), MLA_WIDTH),
        "w_out": nrm(ks[17], (DEPTH, D_MODEL, D_MODEL), D_MODEL),
        "norm2": gain(ks[18], (DEPTH, D_MODEL)),
        "w_up": nrm(ks[19], (DEPTH, D_MODEL, D_FF), D_MODEL),
        "w_down": nrm(ks[20], (DEPTH, D_FF, D_MODEL), D_FF),
        "norm_f": gain(ks[21], (D_MODEL,)),
    }


def reference(x_prompt, x_sample, norm1, w_in, conv_w, conv_b, lru_wa, lru_ba, lru_wx, lru_bx,
              lru_lam, q_norm, w_q_up, kv_norm, w_kv_up, w_lru_proj, w_mla_proj, w_out,
              norm2, w_up, w_down, norm_f):
    y_prompt = trunk(x_prompt, norm1, w_in, conv_w, conv_b, lru_wa, lru_ba, lru_wx, lru_bx,
                     lru_lam, q_norm, w_q_up, kv_norm, w_kv_up, w_lru_proj, w_mla_proj, w_out,
                     norm2, w_up, w_down, norm_f)
    y_sample = trunk(x_sample, norm1, w_in, conv_w, conv_b, lru_wa, lru_ba, lru_wx, lru_bx,
                     lru_lam, q_norm, w_q_up, kv_norm, w_kv_up, w_lru_proj, w_mla_proj, w_out,
                     norm2, w_up, w_down, norm_f)
    return (y_prompt, y_sample)
```

```python
import math
import numpy as np
import concourse.bass as bass
import concourse.mybir as mybir
from concourse.bass_utils import run_bass_kernel_spmd

F32 = mybir.dt.float32
BF16 = mybir.dt.bfloat16
I32 = mybir.dt.int32
ALU = mybir.AluOpType
AF = mybir.ActivationFunctionType
AX = mybir.AxisListType

NCORES = 8
EPS = 1e-6
NT = 512
NSLOT = 8


class Cfg:
    def __init__(self, D=4096, LH=16, MH=32, QL=1024, KVL=512, DFF=16384, SP=8192, SS=2048):
        self.D, self.LH, self.MH, self.QL, self.KVL, self.DFF, self.SP, self.SS = D, LH, MH, QL, KVL, DFF, SP, SS
        self.DC = D // 128
        self.QC = QL // 128
        self.KVC = KVL // 128
        self.FC = DFF // 128
        self.MW = MH * 128
        self.MC = self.MW // 128
        self.TP = SP // NCORES
        self.T = self.TP + SS
        self.INC = (2 * D + QL + KVL + 128 + 2 * D + 255) // 256 * 256
        assert self.TP % NT == 0 and SS % NT == 0 and D % 256 == 0


class T:
    __slots__ = ("name", "ws", "rs", "small", "excl")

    def __init__(self, name, small=False, excl=False):
        self.name, self.ws, self.rs, self.small, self.excl = name, {}, {}, small, excl


class Op:
    __slots__ = ("eng", "fn", "deps", "dma", "slot", "milestone", "count", "noinc", "seq")


ENGS = ("pe", "act", "dve", "pool", "sp")


class Prog:
    def __init__(self, nc):
        self.nc = nc
        self.ops = {e: [] for e in ENGS}
        self.dma_next = {e: 0 for e in ENGS}
        self.dma_cnt = {}
        self.seq = 0

    def add(self, eng, fn, reads=(), writes=(), dma=False, noinc=False):
        ops = self.ops[eng]
        idx = len(ops)
        deps = {}
        if any(t.excl for t in reads):
            writes = list(writes) + [t for t in reads if t.excl and t not in writes]
            reads = [t for t in reads if not t.excl]

        def merge(d, small):
            for k, v in d.items():
                if k == ("c", eng) and eng == "pe":
                    continue
                if deps.get(k, -1) < v:
                    deps[k] = v

        for t in reads:
            merge(t.ws, t.small)
        for t in writes:
            merge(t.ws, t.small)
            merge(t.rs, t.small)
        op = Op()
        op.eng, op.fn, op.deps, op.dma, op.slot, op.milestone, op.count = eng, fn, deps, dma, None, False, 0
        op.noinc, op.seq = noinc, self.seq
        self.seq += 1
        if dma:
            if dma == "cc":
                qn, slot = "cc", 0
            else:
                qn, slot = eng, self.dma_next[eng] % NSLOT
                self.dma_next[eng] += 1
            val = self.dma_cnt.get((qn, slot), 0) + 16
            self.dma_cnt[(qn, slot)] = val
            key = ("d", qn, slot)
            if val > 16:
                deps[key] = max(deps.get(key, 0), val - 16)
            op.slot = (qn, slot)
        else:
            key, val = ("c", eng), idx
        ops.append(op)
        for t in reads:
            if t.rs.get(key, -1) < val:
                t.rs[key] = val
        for t in writes:
            if t.rs:
                t.ws = {key: val}
                t.rs = {}
            elif t.ws.get(key, -1) < val:
                t.ws[key] = val
        return op

    def barrier(self):
        last = {}
        for e in ENGS:
            for i in range(len(self.ops[e]) - 1, -1, -1):
                if not self.ops[e][i].dma:
                    last[("c", e)] = i
                    break
        for (e, s), v in self.dma_cnt.items():
            last[("d", e, s)] = v
        for e in ENGS:
            deps = {k: v for k, v in last.items() if k != ("c", e)}
            op = Op()
            op.eng, op.fn, op.deps, op.dma, op.slot, op.milestone, op.count = e, (lambda en: en.nop()), deps, False, None, False, 0
            op.noinc, op.seq = False, self.seq
            self.seq += 1
            self.ops[e].append(op)

    def emit(self):
        nc = self.nc
        redir = {}
        pe_ops = self.ops["pe"]
        nxt = None
        for i in range(len(pe_ops) - 1, -1, -1):
            if not pe_ops[i].noinc:
                nxt = i
            redir[i] = nxt
        for e in ENGS:
            for op in self.ops[e]:
                for k, v in list(op.deps.items()):
                    if k == ("c", "pe") and redir[v] != v:
                        v2 = redir[v]
                        assert v2 is not None and pe_ops[v2].seq < op.seq, ("open-group milestone cannot be redirected", v, v2)
                        op.deps[k] = v2
        for e in ENGS:
            for op in self.ops[e]:
                for k, v in op.deps.items():
                    if k[0] == "c":
                        self.ops[k[1]][v].milestone = True
        for e in ENGS:
            c = 0
            for op in self.ops[e]:
                if op.milestone:
                    c += 1
                op.count = c
        csem = {e: nc.alloc_semaphore(name="c_" + e) for e in ENGS}
        dsem = {k: nc.alloc_semaphore(name="d_%s_%d" % k) for k in self.dma_cnt}
        allops = self.ops
        dma_cnt = self.dma_cnt

        def stream(e_name, en):
            waited = {}
            for op in allops[e_name]:
                for k, v in op.deps.items():
                    if k[0] == "c":
                        sem, val = csem[k[1]], allops[k[1]][v].count
                    else:
                        sem, val = dsem[(k[1], k[2])], v
                    if waited.get(k, 0) >= val:
                        continue
                    en.wait_ge(sem, val)
                    waited[k] = val
                ins = op.fn(en)
                if op.dma:
                    ins.then_inc(dsem[op.slot], 16)
                elif op.milestone:
                    ins.then_inc(csem[e_name], 1)
            for (e2, s), v in dma_cnt.items():
                if (e2 == e_name or (e2 == "cc" and e_name == "pool")) and waited.get(("d", e2, s), 0) < v:
                    en.wait_ge(dsem[(e2, s)], v)

        with nc.Block() as block:
            @block.tensor
            def _(en):
                stream("pe", en)

            @block.scalar
            def _(en):
                stream("act", en)

            @block.vector
            def _(en):
                stream("dve", en)

            @block.gpsimd
            def _(en):
                stream("pool", en)

            @block.sync
            def _(en):
                stream("sp", en)


class SB:
    def __init__(self, ap, nwords):
        self.ap, self.n, self.off = ap, nwords, 0

    def mark(self):
        return self.off

    def release(self, m):
        self.off = m

    def f32(self, n, parts=128):
        o = self.off
        self.off += n
        assert self.off <= self.n, ("SBUF overflow", self.off, self.n)
        return self.ap[0:parts, o:o + n]

    def bf16(self, n, parts=128):
        w = (n + 1) // 2
        o = self.off
        self.off += w
        assert self.off <= self.n, ("SBUF overflow", self.off, self.n)
        return self.ap[0:parts, o:o + w].bitcast(BF16)[:, 0:n]

    def i32(self, n, parts=128):
        return self.f32(n, parts).bitcast(I32)


def r3(ap, a):
    return ap.rearrange("p (a b) -> p a b", a=a)


def pretile(W, KCt, MWt):
    K, N = W.shape
    KG, NG = K // (KCt * 128), N // MWt
    assert KG * KCt * 128 == K and NG * MWt == N, (W.shape, KCt, MWt)
    return np.ascontiguousarray(
        W.reshape(KG, KCt, 128, NG, MWt).transpose(0, 3, 2, 1, 4)).reshape(KG * NG * 128, KCt * MWt)


class WSpec:
    def __init__(self, name, K, N, KCt, MWt):
        self.name, self.K, self.N, self.KCt, self.MWt = name, K, N, KCt, MWt
        self.KG, self.NG = K // (KCt * 128), N // MWt
        self.rows, self.cols = self.KG * self.NG * 128, KCt * MWt


def weight_specs(c):
    kq = c.QC
    kv = c.KVC
    return [
        WSpec("in", c.D, c.INC, c.DC, 16384 // (c.DC * 2) if c.DC >= 32 else 128),
        WSpec("qup", c.QL, c.MH * 256, kq, min(8192 // kq, c.MH * 256)),
        WSpec("kvk", c.KVL, c.MW, kv, min(8192 // kv, c.MW)),
        WSpec("kvv", c.KVL, c.MW, kv, min(8192 // kv, c.MW)),
        WSpec("ga", 256, 4 * c.LH * 256, 2, 1024),
        WSpec("lru", c.D, c.D, c.DC, 16384 // (c.DC * 2) if c.DC >= 32 else 128),
        WSpec("mla", c.MW, c.D, c.MC, 16384 // (c.MC * 2) if c.MC >= 32 else 128),
        WSpec("out", c.D, c.D, min(c.DC, 16), 512),
        WSpec("up", c.D, c.DFF, c.DC, 16384 // (c.DC * 2) if c.DC >= 32 else 128),
        WSpec("down", c.DFF, c.D, min(16, c.FC), 512),
    ]


WSLOT_ELEMS = 8192


def build(c, debug=(), stop=None):
    nc = bass.Bass("TRN2", target_bir_lowering=False)
    P = Prog(nc)
    D, DC, T_, TP, SS, SP = c.D, c.DC, c.T, c.TP, c.SS, c.SP
    specs = {s.name: s for s in weight_specs(c)}
    for s in specs.values():
        assert s.cols <= WSLOT_ELEMS, (s.name, s.cols)

    def din(name, shape, dt=F32):
        return nc.dram_tensor(name, list(shape), dt, kind="ExternalInput").ap()

    def dscr(name, shape, dt):
        kind = "ExternalOutput" if name in debug else "Internal"
        return nc.dram_tensor(name, list(shape), dt, kind=kind).ap()

    xown = din("xown", [T_, D])
    xp = din("xp", [SP, D])
    xhalo = din("xhalo", [4, D])
    NPV = 2 * DC + 4 * DC + DC + 6 * DC + c.QC + c.KVC
    pvec = din("pvec", [128, NPV])
    cvec = din("cvec", [128, 24])
    normf = din("normf", [1, D])
    wf = {n: din("w_" + n, [s.rows, s.cols]) for n, s in specs.items()}
    wb = {n: dscr("wb_" + n, [s.rows, s.cols], BF16) for n, s in specs.items()}
    y_out = nc.dram_tensor("y", [T_, D], F32, kind="ExternalOutput").ap()

    XLW = T_ + 8
    XL = dscr("XL", [DC, 128, XLW], F32)
    XLp = dscr("XLp", [DC, 128, SP + 4], F32)
    GY = dscr("GY", [DC, 128, T_], BF16)
    GG = dscr("GG", [2 * DC, 128, T_], BF16)
    QN = dscr("QN", [c.MH, 128, T_], BF16)
    QR = dscr("QR", [c.MH, 64, T_], BF16)
    KNs = dscr("KNs", [c.MH, 128, SS], BF16)
    KRs = dscr("KRs", [64, SS], BF16)
    Vs = dscr("Vs", [SS, c.MW], BF16)
    KNp = dscr("KNp", [c.MH, 128, SP], BF16)
    KRp = dscr("KRp", [64, SP], BF16)
    Vp = dscr("Vp", [SP, c.MW], BF16)
    HY = dscr("HY", [DC, 128, T_], BF16)
    OM = dscr("OM", [c.MC, 128, T_], BF16)
    HR = dscr("HR", [T_, D], F32)

    SBW = 53200
    with (nc.sbuf_tensor("sb", [128, SBW], F32) as sbt, nc.psum_tensor("ps", [128, 8, 512], F32) as pst):
        sb = SB(sbt, SBW)
        PS = [T("ps%d" % i, excl=True) for i in range(8)]
        psb = [pst[:, i, :] for i in range(8)]

        ident_f = sb.f32(128)
        ident = sb.bf16(128)
        ones = sb.bf16(128)
        pv = sb.f32(NPV)
        cv = sb.f32(24)
        iota_t = sb.f32(NT, 64)
        rsc = sb.f32(4, 64)
        tC = T("consts")
        tCs = T("consts_small", small=True)
        o = 0
        pv_norm1 = pv[:, o:o + DC]; o += DC
        pv_norm2 = pv[:, o:o + DC]; o += DC
        pv_convw = [pv[:, o + k * DC:o + (k + 1) * DC] for k in range(4)]; o += 4 * DC
        pv_convb = pv[:, o:o + DC]; o += DC
        pv_ba = [pv[:, o + k * DC:o + (k + 1) * DC] for k in range(2)]; o += 2 * DC
        pv_bx = [pv[:, o + k * DC:o + (k + 1) * DC] for k in range(2)]; o += 2 * DC
        pv_lam = [pv[:, o + k * DC:o + (k + 1) * DC] for k in range(2)]; o += 2 * DC
        pv_qn = pv[:, o:o + c.QC]; o += c.QC
        pv_kvn = pv[:, o:o + c.KVC]; o += c.KVC
        assert o == NPV

        P.add("sp", lambda e: e.dma_start(out=pv, in_=pvec[:, :]), writes=[tCs], dma=True)
        P.add("sp", lambda e: e.dma_start(out=cv, in_=cvec[:, :]), writes=[tCs], dma=True)

        tI = T("ident_f")
        P.add("pool", lambda e: e.memset(ident_f, 0.0), writes=[tI])
        P.add("pool", lambda e: e.affine_select(out=ident_f, in_=ident_f, pattern=[[-1, 128]], compare_op=ALU.not_equal,
                                                fill=1.0, base=0, channel_multiplier=1), reads=[tI], writes=[tI])
        P.add("pool", lambda e: e.iota(iota_t, pattern=[[1, NT]], base=0, channel_multiplier=0,
                                       allow_small_or_imprecise_dtypes=True), writes=[tC])
        P.add("dve", lambda e: e.tensor_copy(out=ident, in_=ident_f), reads=[tI], writes=[tC])
        tO = T("ones")
        P.add("dve", lambda e: e.memset(ones, 1.0), writes=[tO])
        TWO_PI = 2.0 * math.pi

        tR0, tR1 = T("rsc0"), T("rsc1")
        P.add("dve", lambda e: e.tensor_scalar(out=rsc[:, 0:1], in0=cv[0:64, 0:1], scalar1=1.0 / TWO_PI, scalar2=None, op0=ALU.mult),
              reads=[tCs], writes=[tR0])
        P.add("dve", lambda e: e.memset(rsc[0:32, 1:2], -TWO_PI * (1.0 - 2e-6)), writes=[tR1])
        P.add("dve", lambda e: e.memset(rsc[32:64, 1:2], TWO_PI * (1.0 - 2e-6)), writes=[tR1])

        TW = {n: T("wb_" + n) for n in specs}
        order = ["in", "qup", "kvk", "kvv", "ga", "lru", "mla", "out", "up", "down"]
        conv_pending = []
        for n in order:
            s = specs[n]
            rows_per = max(128, (8 * 1024 * 1024 // (s.cols * 4)) // 128 * 128)
            r = 0
            while r < s.rows:
                r2 = min(s.rows, r + rows_per)
                conv_pending.append((n, r, r2))
                r = r2

        def conv_some(k=None, upto=None):
            while conv_pending and (k is None or k > 0):
                if upto is not None and conv_pending[0][0] not in upto:
                    break
                n, r, r2 = conv_pending.pop(0)
                P.add("pool", (lambda e, n=n, r=r, r2=r2: e.dma_start(out=wb[n][r:r2, :], in_=wf[n][r:r2, :])),
                      writes=[TW[n]], dma=True)
                if k is not None:
                    k -= 1
        conv_some(upto=("in", "qup", "kvk", "kvv", "ga"))

        NWS = 3
        wslots = []
        wT = []
        wctr = [0]

        def set_wslots(nelems):
            wslots[:] = [sb.bf16(nelems) for _ in range(NWS)]
            wT[:] = [T("wslot%d_%d" % (i, wctr[0])) for i in range(NWS)]

        def wload(name, kg, ng):
            s = specs[name]
            i = wctr[0] % NWS
            wctr[0] += 1
            t = kg * s.NG + ng
            view = wslots[i][:, 0:s.cols]
            P.add("sp", (lambda e, name=name, t=t, view=view: e.dma_start(out=view, in_=wb[name][t * 128:(t + 1) * 128, :])),
                  reads=[TW[name]], writes=[wT[i]], dma=True)
            return wT[i], r3(view, s.KCt)

        m1 = sb.mark()
        set_wslots(WSLOT_ELEMS)
        xbuf = [sb.f32(D) for _ in range(1)]
        xT = [T("xbuf%d" % i) for i in range(1)]
        nbuf = sb.bf16(D)
        nTt = T("nbuf")
        nT = [r3(sb.bf16(DC * NT), DC) for _ in range(2)]
        nTT = [T("nT%d" % i) for i in range(2)]
        small = sb.f32(64)
        smallT = [T("small%d" % i, small=True) for i in range(16)]
        stg = [sb.f32(4 * NT) for _ in range(3)]
        stgT = [T("stg%d" % i) for i in range(3)]
        stgc = [0]
        cq = r3(sb.bf16(c.QC * NT), c.QC)
        ckv = r3(sb.bf16(c.KVC * NT), c.KVC)
        sq = sb.bf16(NT)
        cqT, ckvT, sqT = T("cq"), T("ckv"), T("sq")
        serT = T("ser")
        rstdq = sb.f32(NT)
        rstdkv = sb.f32(NT)
        rstdtok = sb.f32(4)
        rqT, rkT, rtT = T("rstdq"), T("rstdkv"), T("rstdtok", small=True)
        cos2 = sb.f32(NT, 64)
        sins = sb.f32(NT, 64)
        cosq = sb.f32(NT, 64)
        sinq = sb.f32(NT, 64)
        ropeT, ropeqT = T("rope"), T("ropeq")
        ru = sb.f32(NT, 64)
        rk = sb.i32(NT, 64)
        rkf = sb.f32(NT, 64)
        rtmpT = T("ropetmp")
        t1 = [sb.f32(NT) for _ in range(2)]
        t2 = [sb.f32(NT) for _ in range(2)]
        t12T = [T("t12_%d" % i) for i in range(2)]
        SCALE = 192.0 ** -0.5

        pm_ctr = [0]
        cvc = [0]

        def next_pm():
            i = pm_ctr[0] % 4
            pm_ctr[0] += 1
            return 2 + i

        def get_stg():
            i = stgc[0] % 3
            stgc[0] += 1
            return i

        def prep_tile(src, row0, slot, tile_idx):
            for s_ in range(NT // 128):
                xi = 0
                r0 = row0 + s_ * 128
                P.add("sp", (lambda e, xi=xi, r0=r0: e.dma_start(out=xbuf[xi], in_=src[r0:r0 + 128, :])),
                      writes=[xT[xi]], dma=True)
                ssq = small[:, 0:1]
                rst = small[:, 1:2]
                P.add("act", (lambda e, xi=xi: e.activation(out=nbuf, in_=xbuf[xi], func=AF.Square, accum_out=ssq)),
                      reads=[xT[xi]], writes=[nTt, smallT[0]])
                P.add("act", (lambda e: e.activation(out=rst, in_=ssq, func=AF.Sqrt, scale=1.0 / D, bias=EPS)),
                      reads=[smallT[0]], writes=[smallT[1]])
                P.add("dve", (lambda e: e.reciprocal(out=rst, in_=rst)), reads=[smallT[1]], writes=[smallT[1]])
                P.add("act", (lambda e, xi=xi: e.activation(out=nbuf, in_=xbuf[xi], func=AF.Copy, scale=rst)),
                      reads=[xT[xi], smallT[1]], writes=[nTt])
                for g in range(0, DC, 8):
                    ng = min(8, DC - g)
                    bank = (g // 8) % 2
                    pbf = psb[bank].bitcast(BF16)

                    def tr(e, g=g, ng=ng, pbf=pbf):
                        ins = None
                        for j in range(ng):
                            ins = e.transpose(pbf[:, j * 128:(j + 1) * 128], nbuf[:, (g + j) * 128:(g + j + 1) * 128], ident)
                        return ins
                    P.add("pe", tr, reads=[nTt, tC], writes=[PS[bank]])

                    def ev(e, g=g, ng=ng, pbf=pbf, s_=s_):
                        return e.tensor_tensor(
                            out=nT[slot][:, g:g + ng, s_ * 128:(s_ + 1) * 128],
                            in0=r3(pbf[:, 0:ng * 128], ng),
                            in1=pv_norm1[:, g:g + ng].unsqueeze(2).to_broadcast([128, ng, 128]),
                            op=ALU.mult)
                    P.add("dve", ev, reads=[PS[bank], tCs], writes=[nTT[slot]])

        def rope_tables(p0, posoff_col=None):
            if posoff_col is None:
                P.add("dve", lambda e: e.tensor_scalar(out=ru, in0=iota_t, scalar1=float(p0), scalar2=rsc[:, 0:1], op0=ALU.add, op1=ALU.mult),
                      reads=[tC, tR0], writes=[rtmpT])
            else:
                P.add("dve", lambda e: e.tensor_scalar(out=ru, in0=iota_t, scalar1=cv[0:64, posoff_col:posoff_col + 1], scalar2=float(p0),
                                                       op0=ALU.add, op1=ALU.add), reads=[tC, tCs], writes=[rtmpT])
                P.add("dve", lambda e: e.tensor_scalar(out=ru, in0=ru, scalar1=rsc[:, 0:1], scalar2=None, op0=ALU.mult),
                      reads=[rtmpT, tR0], writes=[rtmpT])
            for which in range(2):
                if which == 1:
                    P.add("dve", lambda e: e.tensor_scalar(out=ru, in0=ru, scalar1=0.25, scalar2=None, op0=ALU.add), reads=[rtmpT], writes=[rtmpT])
                P.add("dve", lambda e: e.tensor_copy(out=rk, in_=ru), reads=[rtmpT], writes=[rtmpT])
                P.add("dve", lambda e: e.tensor_copy(out=rkf, in_=rk), reads=[rtmpT], writes=[rtmpT])
                P.add("dve", lambda e: e.tensor_tensor(out=rkf, in0=ru, in1=rkf, op=ALU.subtract), reads=[rtmpT], writes=[rtmpT])
                if which == 0:
                    P.add("act", lambda e: e.activation(out=sins, in_=rkf, func=AF.Sin, scale=rsc[:, 1:2]),
                          reads=[rtmpT, tR1], writes=[ropeT])
                else:
                    P.add("act", lambda e: e.activation(out=cos2, in_=rkf, func=AF.Sin, scale=TWO_PI * (1.0 - 2e-6)),
                          reads=[rtmpT], writes=[ropeT])

        def mm_group(lhs_fn, rhs_fn, KC, bank, M=128, N=NT, extra_reads=()):
            def f(e):
                ins = None
                for k in range(KC):
                    ins = e.matmul(psb[bank][0:M, 0:N], lhsT=lhs_fn(k), rhs=rhs_fn(k), start=(k == 0), stop=(k == KC - 1))
                return ins
            return f

        def store(dst_ap, src_ap, reads, writes, slow=False):
            P.add("pool", (lambda e: e.dma_start(out=dst_ap, in_=src_ap, allow_slow_non_contiguous=slow)), reads=reads, writes=writes, dma=True)

        TXL, TGY, TGG, TQN, TQR = T("XL"), T("GY"), T("GG"), T("QN"), T("QR")
        TXLp = T("XLp")
        TK = {"s": (T("KNs"), T("KRs"), T("Vs")), "p": (T("KNp"), T("KRp"), T("Vp"))}

        def in_proj_tile(slot, kind, tcol, xlcol, pos0, posoff_col, kvdst, kvcol, prep_next=None):
            s_in = specs["in"]
            MWt = s_in.MWt
            nblk_tile = MWt // 128
            import os
            do_main = kind in ("prompt", "sample")
            do_kv = kind in ("ctx", "sample")
            do_q = do_main
            dq = do_q and os.environ.get("KDBG", "") != "noq"
            if os.environ.get("KDBG", "") == "nomain":
                do_main = False
            rope_tables(pos0, posoff_col)
            b_xl, b_gy, b_cq, b_ckv, b_kr, b_gg = 0, DC, 2 * DC, 2 * DC + c.QC, 2 * DC + c.QC + c.KVC, 2 * DC + c.QC + c.KVC + 1
            nblocks = b_gg + 2 * DC
            blocks = []
            XLd = XLp if kind == "ctx" else XL
            for b in range(nblocks):
                if b < b_gy:
                    if do_main or kind == "ctx":
                        blocks.append(b)
                elif b < b_cq:
                    if do_main:
                        blocks.append(b)
                elif b < b_ckv:
                    if do_q:
                        blocks.append(b)
                elif b < b_gg:
                    if do_kv:
                        blocks.append(b)
                elif do_main:
                    blocks.append(b)
            cur_tile = [None, None, None]
            pend = {}
            KSKIP = os.environ.get("KSKIP", "").split(",")
            if "xl" in KSKIP:
                blocks = [b for b in blocks if not b < b_gy]
            if "gy" in KSKIP:
                blocks = [b for b in blocks if not (b_gy <= b < b_cq)]
            if "cq" in KSKIP:
                blocks = [b for b in blocks if not (b_cq <= b < b_ckv)]
            if "gg" in KSKIP:
                blocks = [b for b in blocks if not (b >= b_gg)]

            def flush(key):
                if key not in pend:
                    return
                si, cnt, b0 = pend.pop(key)
                if key == "xl":
                    src = r3(stg[si], 4)[:, 0:cnt, :]
                    dst = XLd[b0:b0 + cnt, :, xlcol:xlcol + NT].rearrange("m p t -> p m t")
                    store(dst, src, [stgT[si]], [TXLp if kind == "ctx" else TXL])
                elif key == "gy":
                    src = r3(stg[si].bitcast(BF16), 8)[:, 0:cnt, :]
                    dst = GY[b0 - b_gy:b0 - b_gy + cnt, :, tcol:tcol + NT].rearrange("m p t -> p m t")
                    store(dst, src, [stgT[si]], [TGY])
                elif key == "gg":
                    src = r3(stg[si].bitcast(BF16), 8)[:, 0:cnt, :]
                    dst = GG[b0 - b_gg:b0 - b_gg + cnt, :, tcol:tcol + NT].rearrange("m p t -> p m t")
                    store(dst, src, [stgT[si]], [TGG])

            def stage(key, b, cap):
                if key in pend and pend[key][1] == cap:
                    flush(key)
                if key not in pend:
                    pend[key] = [get_stg(), 0, b]
                ent = pend[key]
                pos = ent[1]
                ent[1] += 1
                return ent[0], pos

            for bi, b in enumerate(blocks):
                if prep_next is not None and bi == min(len(blocks) - 1, 24):
                    prep_next()
                pass
                tix = b // nblk_tile
                if cur_tile[0] != tix:
                    wt, wv = wload("in", 0, tix)
                    cur_tile[0], cur_tile[1], cur_tile[2] = tix, wt, wv
                wt, wv = cur_tile[1], cur_tile[2]
                co = (b % nblk_tile) * 128
                is_kr = (b == b_kr)
                if not is_kr:
                    bank = next_pm()
                    P.add("pe", mm_group(lambda k, wv=wv, co=co: wv[:, k, co:co + 128], lambda k: nT[slot][:, k, :], DC, bank),
                          reads=[wt, nTT[slot]], writes=[PS[bank]])
                if b < b_gy:
                    si, pos = stage("xl", b, 4)
                    P.add("act", (lambda e, si=si, pos=pos, bank=bank: e.activation(out=r3(stg[si], 4)[:, pos, :], in_=psb[bank], func=AF.Copy)),
                          reads=[PS[bank]], writes=[stgT[si]])
                elif b < b_cq:
                    si, pos = stage("gy", b, 8)
                    P.add("act", (lambda e, si=si, pos=pos, bank=bank: e.activation(out=r3(stg[si].bitcast(BF16), 8)[:, pos, :], in_=psb[bank], func=AF.Gelu_apprx_tanh)),
                          reads=[PS[bank]], writes=[stgT[si]])
                elif b < b_ckv:
                    j = b - b_cq
                    P.add("act", (lambda e, bank=bank: e.activation(out=sq, in_=psb[bank], func=AF.Square)),
                          reads=[PS[bank]], writes=[sqT])
                    P.add("dve", (lambda e, j=j, bank=bank: e.tensor_scalar(out=cq[:, j, :], in0=psb[bank], scalar1=pv_qn[:, j:j + 1], scalar2=None, op0=ALU.mult)),
                          reads=[PS[bank], tCs], writes=[cqT])
                    P.add("pe", (lambda e: e.matmul(psb[7], lhsT=ones, rhs=sq, start=True, stop=True)),
                          reads=[sqT, tO], writes=[PS[7]])
                    if j == 0:
                        P.add("dve", (lambda e: e.tensor_copy(out=rstdq, in_=psb[7])), reads=[PS[7]], writes=[rqT])
                    else:
                        P.add("dve", (lambda e: e.tensor_tensor(out=rstdq, in0=psb[7], in1=rstdq, op=ALU.add)), reads=[PS[7], rqT], writes=[rqT])
                    if j == c.QC - 1 and "rstd" not in KSKIP:
                        P.add("act", (lambda e: e.activation(out=rstdq, in_=rstdq, func=AF.Sqrt, scale=1.0 / c.QL, bias=EPS)),
                              reads=[rqT], writes=[rqT])
                        P.add("dve", (lambda e: e.reciprocal(out=rstdq, in_=rstdq)), reads=[rqT], writes=[rqT])
                        if "poolq" not in KSKIP:
                          P.add("pool", (lambda e: e.tensor_scalar(out=rstdq, in0=rstdq, scalar1=SCALE, scalar2=None, op0=ALU.mult)),
                              reads=[rqT], writes=[rqT])
                        if "poolq2" not in KSKIP:
                          P.add("pool", (lambda e: e.tensor_tensor(out=cosq, in0=cos2, in1=rstdq[0:64, :], op=ALU.mult)),
                              reads=[rqT, ropeT], writes=[ropeqT])
                          P.add("pool", (lambda e: e.tensor_tensor(out=sinq, in0=sins, in1=rstdq[0:64, :], op=ALU.mult)),
                              reads=[rqT, ropeT], writes=[ropeqT])
                elif b < b_kr:
                    j = b - b_ckv
                    P.add("act", (lambda e, bank=bank: e.activation(out=sq, in_=psb[bank], func=AF.Square)),
                          reads=[PS[bank]], writes=[sqT])
                    P.add("dve", (lambda e, j=j, bank=bank: e.tensor_scalar(out=ckv[:, j, :], in0=psb[bank], scalar1=pv_kvn[:, j:j + 1], scalar2=None, op0=ALU.mult)),
                          reads=[PS[bank], tCs], writes=[ckvT])
                    P.add("pe", (lambda e: e.matmul(psb[7], lhsT=ones, rhs=sq, start=True, stop=True)),
                          reads=[sqT, tO], writes=[PS[7]])
                    if j == 0:
                        P.add("dve", (lambda e: e.tensor_copy(out=rstdkv, in_=psb[7])), reads=[PS[7]], writes=[rkT])
                    else:
                        P.add("dve", (lambda e: e.tensor_tensor(out=rstdkv, in0=psb[7], in1=rstdkv, op=ALU.add)), reads=[PS[7], rkT], writes=[rkT])
                    def tokss(e, j=j):
                        ins = None
                        for s_ in range(4):
                            ins = e.matmul(psb[6][:, s_:s_ + 1], lhsT=sq[:, s_ * 128:(s_ + 1) * 128], rhs=ones[:, 0:1], start=True, stop=True)
                        return ins
                    P.add("pe", tokss, reads=[sqT, tO], writes=[PS[6]])
                    if j == 0:
                        P.add("dve", (lambda e: e.tensor_copy(out=rstdtok, in_=psb[6][:, 0:4])), reads=[PS[6]], writes=[rtT])
                    else:
                        P.add("dve", (lambda e: e.tensor_tensor(out=rstdtok, in0=psb[6][:, 0:4], in1=rstdtok, op=ALU.add)), reads=[PS[6], rtT], writes=[rtT])
                    if j == c.KVC - 1:
                        P.add("act", (lambda e: e.activation(out=rstdkv, in_=rstdkv, func=AF.Sqrt, scale=1.0 / c.KVL, bias=EPS)),
                              reads=[rkT], writes=[rkT])
                        P.add("dve", (lambda e: e.reciprocal(out=rstdkv, in_=rstdkv)), reads=[rkT], writes=[rkT])
                        P.add("act", (lambda e: e.activation(out=rstdtok, in_=rstdtok, func=AF.Sqrt, scale=1.0 / c.KVL, bias=EPS)),
                              reads=[rtT], writes=[rtT])
                        P.add("dve", (lambda e: e.reciprocal(out=rstdtok, in_=rstdtok)), reads=[rtT], writes=[rtT])
                elif is_kr:
                    ba_, bb_ = next_pm(), next_pm()
                    for bank, c0 in ((ba_, co), (bb_, co + 64)):
                        P.add("pe", mm_group(lambda k, wv=wv, c0=c0: wv[:, k, c0:c0 + 64], lambda k: nT[slot][:, k, :], DC, bank, M=64),
                              reads=[wt, nTT[slot]], writes=[PS[bank]])
                    ti = 0
                    P.add("dve", (lambda e, ba_=ba_, ti=ti: e.tensor_tensor(out=t1[ti][0:64, :], in0=psb[ba_][0:64, :], in1=cos2, op=ALU.mult)),
                          reads=[PS[ba_], ropeT], writes=[t12T[ti]])
                    P.add("dve", (lambda e, bb_=bb_, ti=ti: e.tensor_tensor(out=t2[ti][0:64, :], in0=psb[bb_][0:64, :], in1=sins, op=ALU.mult)),
                          reads=[PS[bb_], ropeT], writes=[t12T[ti]])
                    si = get_stg()
                    P.add("pool", (lambda e, si=si, ti=ti: e.tensor_tensor(out=stg[si].bitcast(BF16)[0:64, 0:NT], in0=t1[ti][0:64, :], in1=t2[ti][0:64, :], op=ALU.add)),
                          reads=[t12T[ti]], writes=[stgT[si]])
                    store(kvdst[1][:, kvcol:kvcol + NT], stg[si].bitcast(BF16)[0:64, 0:NT], [stgT[si]], [kvdst[4]])
                else:
                    si, pos = stage("gg", b, 8)
                    P.add("act", (lambda e, si=si, pos=pos, bank=bank: e.activation(out=r3(stg[si].bitcast(BF16), 8)[:, pos, :], in_=psb[bank], func=AF.Sigmoid)),
                          reads=[PS[bank]], writes=[stgT[si]])
                if b == b_gy - 1 or (kind == "ctx" and b == b_gy - 1):
                    flush("xl")
                if b == b_cq - 1:
                    flush("gy")
            flush("gg")

            if dq:
                sq_ = specs["qup"]
                hp = sq_.MWt // 256
                for h in range(c.MH):
                    if h % hp == 0:
                        wt, wv = wload("qup", 0, h // hp)
                    co = (h % hp) * 256
                    bank = next_pm()
                    P.add("pe", mm_group(lambda k, wv=wv, co=co: wv[:, k, co:co + 128], lambda k: cq[:, k, :], c.QC, bank),
                          reads=[wt, cqT], writes=[PS[bank]])
                    if h % 8 == 0:
                        sn = get_stg()
                    pos = h % 8
                    P.add("dve", (lambda e, sn=sn, pos=pos, bank=bank: e.tensor_tensor(out=r3(stg[sn].bitcast(BF16), 8)[:, pos, :], in0=psb[bank], in1=rstdq, op=ALU.mult)),
                          reads=[PS[bank], rqT], writes=[stgT[sn]])
                    ba_, bb_ = next_pm(), next_pm()
                    for bank2, c0 in ((ba_, co + 128), (bb_, co + 192)):
                        P.add("pe", mm_group(lambda k, wv=wv, c0=c0: wv[:, k, c0:c0 + 64], lambda k: cq[:, k, :], c.QC, bank2, M=64),
                              reads=[wt, cqT], writes=[PS[bank2]])
                    ti = h % 2
                    P.add("dve", (lambda e, ba_=ba_, ti=ti: e.tensor_tensor(out=t1[ti][0:64, :], in0=psb[ba_][0:64, :], in1=cosq, op=ALU.mult)),
                          reads=[PS[ba_], ropeqT], writes=[t12T[ti]])
                    P.add("dve", (lambda e, bb_=bb_, ti=ti: e.tensor_tensor(out=t2[ti][0:64, :], in0=psb[bb_][0:64, :], in1=sinq, op=ALU.mult)),
                          reads=[PS[bb_], ropeqT], writes=[t12T[ti]])
                    if h % 8 == 0:
                        sr = get_stg()
                    P.add("pool", (lambda e, sr=sr, pos=pos, ti=ti: e.tensor_tensor(out=r3(stg[sr].bitcast(BF16), 8)[0:64, pos, :], in0=t1[ti][0:64, :], in1=t2[ti][0:64, :], op=ALU.add)),
                          reads=[t12T[ti]], writes=[stgT[sr]])
                    if h % 8 == 7 or h == c.MH - 1:
                        h0 = h - pos
                        cnt = pos + 1
                        store(QN[h0:h0 + cnt, :, tcol:tcol + NT].rearrange("m p t -> p m t"), r3(stg[sn].bitcast(BF16), 8)[:, 0:cnt, :], [stgT[sn]], [TQN])
                        store(QR[h0:h0 + cnt, :, tcol:tcol + NT].rearrange("m p t -> p m t"), r3(stg[sr].bitcast(BF16), 8)[0:64, 0:cnt, :], [stgT[sr]], [TQR])

            if do_kv:
                KNd, KRd, Vd, tKN, tKR, tV = kvdst
                sk = specs["kvk"]
                hp = sk.MWt // 128
                for h in range(c.MH):
                    if h % hp == 0:
                        wt, wv = wload("kvk", 0, h // hp)
                    co = (h % hp) * 128
                    bank = next_pm()
                    P.add("pe", mm_group(lambda k, wv=wv, co=co: wv[:, k, co:co + 128], lambda k: ckv[:, k, :], c.KVC, bank),
                          reads=[wt, ckvT], writes=[PS[bank]])
                    if h % 8 == 0:
                        sn = get_stg()
                    pos = h % 8
                    P.add("dve", (lambda e, sn=sn, pos=pos, bank=bank: e.tensor_tensor(out=r3(stg[sn].bitcast(BF16), 8)[:, pos, :], in0=psb[bank], in1=rstdkv, op=ALU.mult)),
                          reads=[PS[bank], rkT], writes=[stgT[sn]])
                    if h % 8 == 7 or h == c.MH - 1:
                        h0 = h - pos
                        cnt = pos + 1
                        store(KNd[h0:h0 + cnt, :, kvcol:kvcol + NT].rearrange("m p t -> p m t"), r3(stg[sn].bitcast(BF16), 8)[:, 0:cnt, :], [stgT[sn]], [tKN])
                sv = specs["kvv"]
                for ng in range(sv.NG):
                    wt, wv = wload("kvv", 0, ng)
                    for n0 in range(0, sv.MWt, 512):
                        sn = get_stg()
                        for s_ in range(4):
                            bank = next_pm()
                            P.add("pe", mm_group(lambda k, s_=s_: ckv[:, k, s_ * 128:(s_ + 1) * 128], lambda k, wv=wv, n0=n0: wv[:, k, n0:n0 + 512], c.KVC, bank),
                                  reads=[wt, ckvT], writes=[PS[bank]])
                            P.add("act", (lambda e, sn=sn, s_=s_, bank=bank: e.activation(out=r3(stg[sn].bitcast(BF16), 8)[:, s_, :], in_=psb[bank], func=AF.Copy, scale=rstdtok[:, s_:s_ + 1])),
                                  reads=[PS[bank], rtT], writes=[stgT[sn]])
                        cols = ng * sv.MWt + n0
                        store(Vd[kvcol:kvcol + NT, cols:cols + 512].rearrange("(s p) n -> p s n", p=128), r3(stg[sn].bitcast(BF16), 8)[:, 0:4, :], [stgT[sn]], [tV])

        sched = []
        for i in range(SP // NT):
            sched.append(dict(kind="ctx", src=xp, row0=i * NT, tcol=None, xlcol=2 + i * NT, pos0=i * NT, posoff=None, kv="p", kvcol=i * NT))
        for i in range(TP // NT):
            sched.append(dict(kind="prompt", src=xown, row0=i * NT, tcol=i * NT, xlcol=2 + i * NT, pos0=i * NT, posoff=1, kv=None, kvcol=None))
        for i in range(SS // NT):
            sched.append(dict(kind="sample", src=xown, row0=TP + i * NT, tcol=TP + i * NT, xlcol=TP + 6 + i * NT, pos0=i * NT, posoff=None, kv="s", kvcol=i * NT))

        def kvd(k):
            if k is None:
                return None
            if k == "p":
                return (KNp, KRp, Vp, TK["p"][0], TK["p"][1], TK["p"][2])
            return (KNs, KRs, Vs, TK["s"][0], TK["s"][1], TK["s"][2])

        import os
        if os.environ.get("KSCHED"):
            sched = [x for x in sched if x["kind"] in os.environ["KSCHED"].split(",")]
        if stop == "p0":
            sched = []
        elif stop is not None and stop.startswith("t"):
            sched = sched[:int(stop[1:])]
        if sched:
            prep_tile(sched[0]["src"], sched[0]["row0"], 0, 0)
        if stop == "prep":
            sched = []
        for i, sc in enumerate(sched):
            nxt = None
            if i + 1 < len(sched):
                n_ = sched[i + 1]
                nxt = (lambda n_=n_, i=i: prep_tile(n_["src"], n_["row0"], (i + 1) % 2, i + 1))
            in_proj_tile(i % 2, sc["kind"], sc["tcol"], sc["xlcol"], sc["pos0"], sc["posoff"], kvd(sc["kv"]), sc["kvcol"], prep_next=nxt)

        hx = xbuf[0][0:4, :]
        hn = nbuf[0:4, :]
        hT = r3(sb.bf16(DC * 4), DC)
        hxT, hnT, hTT = xT[0], nTt, T("hT")
        P.add("sp", lambda e: e.dma_start(out=hx, in_=xhalo[:, :]), writes=[hxT], dma=True)
        P.add("act", lambda e: e.activation(out=hn, in_=hx, func=AF.Square, accum_out=small[0:4, 0:1]), reads=[hxT], writes=[hnT, smallT[0]])
        P.add("act", lambda e: e.activation(out=small[0:4, 1:2], in_=small[0:4, 0:1], func=AF.Sqrt, scale=1.0 / D, bias=EPS), reads=[smallT[0]], writes=[smallT[1]])
        P.add("dve", lambda e: e.reciprocal(out=small[0:4, 1:2], in_=small[0:4, 1:2]), reads=[smallT[1]], writes=[smallT[1]])
        P.add("act", lambda e: e.activation(out=hn, in_=hx, func=AF.Copy, scale=small[0:4, 1:2]), reads=[hxT, smallT[1]], writes=[hnT])
        pbf0 = psb[0].bitcast(BF16)
        for g in range(0, DC, 8):
            ng = min(8, DC - g)

            def trh(e, g=g, ng=ng):
                ins = None
                for j in range(ng):
                    ins = e.transpose(pbf0[:, j * 4:(j + 1) * 4], hn[:, (g + j) * 128:(g + j + 1) * 128], ident[0:4, 0:4])
                return ins
            P.add("pe", trh, reads=[hnT, tC], writes=[PS[0]])
            P.add("dve", (lambda e, g=g, ng=ng: e.tensor_tensor(out=hT[:, g:g + ng, :], in0=r3(pbf0[:, 0:ng * 4], ng),
                                                                 in1=pv_norm1[:, g:g + ng].unsqueeze(2).to_broadcast([128, ng, 4]), op=ALU.mult)),
                  reads=[PS[0], tCs], writes=[hTT])
        s_in = specs["in"]
        nblk_tile = s_in.MWt // 128
        hst = sb.f32(4 * DC)
        hstT = T("hst")
        for b in range(DC):
            if b % nblk_tile == 0:
                wt, wv = wload("in", 0, b // nblk_tile)
            co = (b % nblk_tile) * 128
            bank = next_pm()
            P.add("pe", mm_group(lambda k, wv=wv, co=co: wv[:, k, co:co + 128], lambda k: hT[:, k, :], DC, bank, N=4),
                  reads=[wt, hTT], writes=[PS[bank]])
            P.add("act", (lambda e, b=b, bank=bank: e.activation(out=hst[:, b * 4:(b + 1) * 4], in_=psb[bank][:, 0:4], func=AF.Copy)),
                  reads=[PS[bank]], writes=[hstT])
        hst3 = r3(hst, DC)
        store(XL[:, :, 0:2].rearrange("m p t -> p m t"), hst3[:, :, 0:2], [hstT], [TXL], slow=True)
        store(XL[:, :, TP + 2:TP + 3].rearrange("m p t -> p m t"), hst3[:, :, 2:3], [hstT], [TXL], slow=True)

        P.barrier()
        sb.release(m1)

        if stop != "s1":
            m2 = sb.mark()
            set_wslots(specs["ga"].cols)
            WM = max(TP, SS)
            x2_ = r3(sb.f32(2 * (WM + 8)), 2)
            xc_ = r3(sb.f32(2 * WM), 2)
            xcb_ = r3(sb.bf16(2 * WM), 2)
            rr = sb.f32(4 * WM)
            ii = sb.f32(4 * WM)
            tmp = sb.f32(2 * WM)
            HF_ = sb.f32(WM)
            HB_ = sb.f32(WM)
            zer = sb.f32(2 * DC)
            gyb = r3(sb.bf16(2 * WM), 2)
            hyb = r3(sb.bf16(2 * WM), 2)
            cneg = sb.f32(2 * DC)
            c2 = sb.f32(2 * DC)
            ccin = sb.f32(4 * DC)
            ccall = sb.f32(NCORES * 4 * DC)
            car = sb.f32(2 * DC)
            chn = sb.f32(DC)
            zT, gyT, hyT = T("zer"), T("gyb"), T("hyb")
            x2T_, xcT_, xcbT_, HFT_, HBT_ = [[T(n + str(p)) for p in range(2)] for n in ("x2", "xc", "xcb", "HF", "HB")]
            rrT_ = [[T("rr%d_%d" % (p, i)) for i in range(4)] for p in range(2)]
            iiT_ = [[T("ii%d_%d" % (p, i)) for i in range(4)] for p in range(2)]
            tmpT_ = [T("tmp%d" % i) for i in range(4)]
            item = [0]
            cT, ccT, carT, chnT = T("cneg", small=True), T("ccin", small=True), T("car", small=True), T("chn", small=True)
            racc = sb.f32(64)
            raccT_ = [[T("racc%d_%d" % (p, i), small=True) for i in range(4)] for p in range(2)]
            THS, TPF, TPB, THY = T("HS"), T("PF"), T("PB"), T("HY")
            TCC = T("CC")
            for d in range(2):
                P.add("act", (lambda e, d=d: e.activation(out=cneg[:, d * DC:(d + 1) * DC], in_=pv_lam[d], func=AF.Exp, scale=-1.0)), reads=[tCs], writes=[cT])
            P.add("act", lambda e: e.activation(out=cneg, in_=cneg, func=AF.Ln, scale=1.0, bias=1.0), reads=[cT], writes=[cT])
            P.add("dve", lambda e: e.tensor_scalar(out=c2, in0=cneg, scalar1=-16.0, scalar2=None, op0=ALU.mult), reads=[cT], writes=[cT])
            P.add("dve", lambda e: e.tensor_scalar(out=cneg, in0=cneg, scalar1=-8.0, scalar2=None, op0=ALU.mult), reads=[cT], writes=[cT])
            P.add("pool", lambda e: e.memset(zer, 0.0), writes=[zT])
            sga = specs["ga"]
            hpt = sga.MWt // 1024 if sga.MWt >= 1024 else 1

            def ga_cols(d, g, h):
                return ((d * 2 + g) * c.LH + h) * 256

            def lru_region(h, mode, r0, W, t0, jc=None):
                ch = [2 * h, 2 * h + 1]
                half = (W <= WM // 2)
                par = (item[0] % 2) if half else 0
                item[0] += 1
                wo = par * (WM // 2)
                x2 = x2_[:, :, par * (WM // 2 + 4):par * (WM // 2 + 4) + W + 4] if half else x2_
                xc, xcb, HF, HB = xc_[:, :, wo:wo + W], xcb_[:, :, wo:wo + W], HF_[:, wo:wo + W], HB_[:, wo:wo + W]
                x2T, xcT, xcbT, HFT, HBT = x2T_[par], xcT_[par], xcbT_[par], HFT_[par], HBT_[par]
                rrT, iiT = rrT_[par], iiT_[par]
                raccT, ro = raccT_[par], par * 32
                src = XLp if mode == "ctx" else XL
                tsrc = TXLp if mode == "ctx" else TXL
                if mode == "sample":
                    P.add("sp", (lambda e: e.dma_start(out=x2[:, :, 2:W + 2], in_=src[ch[0]:ch[0] + 2, :, r0 + 2:r0 + W + 2].rearrange("m p t -> p m t"))),
                          reads=[tsrc], writes=[x2T], dma=True)
                    P.add("pool", lambda e: e.memset(x2[:, :, 0:2], 0.0), writes=[x2T])
                    P.add("pool", lambda e: e.memset(x2[:, :, W + 2:W + 4], 0.0), writes=[x2T])
                else:
                    P.add("sp", (lambda e: e.dma_start(out=x2[:, :, 0:W + 3], in_=src[ch[0]:ch[0] + 2, :, r0:r0 + W + 3].rearrange("m p t -> p m t"))),
                          reads=[tsrc], writes=[x2T], dma=True)
                if mode != "ctx":
                    P.add("sp", (lambda e: e.dma_start(out=gyb[:, :, 0:W], in_=GY[ch[0]:ch[0] + 2, :, t0:t0 + W].rearrange("m p t -> p m t"))),
                          reads=[TGY], writes=[gyT], dma=True)
                for oc in range(2):
                    cc_ = ch[oc]
                    P.add("act", (lambda e, oc=oc, cc_=cc_: e.activation(out=xc[:, oc, 0:W], in_=x2[:, oc, 0:W], func=AF.Identity,
                                                                          scale=pv_convw[0][:, cc_:cc_ + 1], bias=pv_convb[:, cc_:cc_ + 1])),
                          reads=[x2T, tCs], writes=[xcT])
                    for k in range(1, 4):
                        P.add("dve", (lambda e, oc=oc, cc_=cc_, k=k: e.scalar_tensor_tensor(out=xc[:, oc, 0:W], in0=x2[:, oc, k:k + W], scalar=pv_convw[k][:, cc_:cc_ + 1],
                                                                                           in1=xc[:, oc, 0:W], op0=ALU.mult, op1=ALU.add)),
                              reads=[x2T, xcT, tCs], writes=[xcT])
                P.add("pool", lambda e: e.tensor_copy(out=xcb[:, :, 0:W], in_=xc[:, :, 0:W]), reads=[xcT], writes=[xcbT])
                for d in range(2):
                    for g in range(2):
                        col = ga_cols(d, g, h)
                        tix, cin = col // sga.MWt, col % sga.MWt
                        wt, wv = wload("ga", 0, tix)
                        for oc in range(2):
                            cc_ = ch[oc]
                            dst = (rr if g == 0 else ii)
                            dT = (rrT if g == 0 else iiT)[d * 2 + oc]
                            bvec = (pv_ba if g == 0 else pv_bx)[d][:, cc_:cc_ + 1]
                            for tt in range(W // NT):
                                bank = next_pm()
                                P.add("pe", mm_group(lambda k, wv=wv, cin=cin, oc=oc: wv[:, k, cin + oc * 128:cin + (oc + 1) * 128],
                                                     lambda k, tt=tt: xcb[:, k, tt * NT:(tt + 1) * NT], 2, bank),
                                      reads=[wt, xcbT], writes=[PS[bank]])
                                o_ = (d * 2 + oc) * WM + wo + tt * NT
                                if mode == "ctx" and g == 0:
                                    ai = ro + (d * 2 + oc) * 8 + tt
                                    P.add("act", (lambda e, dst=dst, o_=o_, bank=bank, bvec=bvec, ai=ai: e.activation(out=dst[:, o_:o_ + NT], in_=psb[bank], func=AF.Sigmoid, bias=bvec,
                                                                                                                    accum_out=racc[:, ai:ai + 1])),
                                          reads=[PS[bank], tCs], writes=[dT, raccT[d * 2 + oc]])
                                else:
                                    P.add("act", (lambda e, dst=dst, o_=o_, bank=bank, bvec=bvec: e.activation(out=dst[:, o_:o_ + NT], in_=psb[bank], func=AF.Sigmoid, bias=bvec)),
                                          reads=[PS[bank], tCs], writes=[dT])
                for oc in range(2):
                    cc_ = ch[oc]
                    ent = []
                    for d in range(2):
                        ix = d * 2 + oc
                        rv = rr[:, ix * WM + wo:ix * WM + wo + W]
                        iv = ii[:, ix * WM + wo:ix * WM + wo + W]
                        if half:
                            tsel = par * 2 + d
                            tv = tmp[:, tsel * (WM // 2):tsel * (WM // 2) + W]
                            tvT = tmpT_[tsel]
                        else:
                            tv, tvT = tmp[:, d * WM:d * WM + W], tmpT_[d * 2]
                        ent.append((d, ix, rv, iv, tv, tvT))
                        if mode == "ctx":
                            nacc = W // NT
                            a0 = ro + ix * 8
                            for q_ in range(1, nacc):
                                P.add("dve", (lambda e, a0=a0, q_=q_: e.tensor_tensor(out=racc[:, a0:a0 + 1], in0=racc[:, a0:a0 + 1], in1=racc[:, a0 + q_:a0 + q_ + 1], op=ALU.add)),
                                      reads=[], writes=[raccT[ix]])
                            dstc = (0 if d == 0 else 2 * DC) + cc_
                            P.add("act", (lambda e, a0=a0, d=d, cc_=cc_, dstc=dstc: e.activation(out=call3[:, jc, dstc:dstc + 1], in_=racc[:, a0:a0 + 1], func=AF.Exp,
                                                                                                 scale=cneg[:, d * DC + cc_:d * DC + cc_ + 1])),
                                  reads=[raccT[ix], cT], writes=[ccT])
                        P.add("act", (lambda e, rv=rv, d=d, cc_=cc_, tv=tv: e.activation(out=tv, in_=rv, func=AF.Exp, scale=c2[:, d * DC + cc_:d * DC + cc_ + 1])),
                              reads=[rrT[ix], cT], writes=[tvT])
                        P.add("act", (lambda e, rv=rv, d=d, cc_=cc_: e.activation(out=rv, in_=rv, func=AF.Exp, scale=cneg[:, d * DC + cc_:d * DC + cc_ + 1])),
                              reads=[cT], writes=[rrT[ix]])
                    for (d, ix, rv, iv, tv, tvT) in ent:
                        P.add("act", (lambda e, tv=tv: e.activation(out=tv, in_=tv, func=AF.Sqrt, scale=-1.0, bias=1.0)), reads=[], writes=[tvT])
                    for (d, ix, rv, iv, tv, tvT) in ent:
                        P.add("pool", (lambda e, iv=iv, oc=oc: e.tensor_tensor(out=iv, in0=iv, in1=xc[:, oc, 0:W], op=ALU.mult)), reads=[xcT], writes=[iiT[ix]])
                        P.add("dve", (lambda e, iv=iv, tv=tv: e.tensor_tensor(out=iv, in0=iv, in1=tv, op=ALU.mult)), reads=[tvT], writes=[iiT[ix]])
                    af, uf = rr[:, oc * WM + wo:oc * WM + wo + W], ii[:, oc * WM + wo:oc * WM + wo + W]
                    ab, ub = rr[:, (2 + oc) * WM + wo:(2 + oc) * WM + wo + W], ii[:, (2 + oc) * WM + wo:(2 + oc) * WM + wo + W]
                    if mode == "own":
                        inf_, inb_ = car[:, cc_:cc_ + 1], car[:, DC + cc_:DC + cc_ + 1]
                        rd = [carT]
                    else:
                        inf_, inb_, rd = 0.0, 0.0, []
                    P.add("dve", (lambda e, af=af, uf=uf, inf_=inf_: e.tensor_tensor_scan(out=HF[:, 0:W], data0=af, data1=uf, initial=inf_, op0=ALU.mult, op1=ALU.add)),
                          reads=[rrT[oc], iiT[oc]] + rd, writes=[HFT])
                    P.add("dve", (lambda e, ab=ab, ub=ub, inb_=inb_: e.tensor_tensor_scan(out=HB[:, 0:W][:, ::-1], data0=ab[:, ::-1], data1=ub[:, ::-1], initial=inb_, op0=ALU.mult, op1=ALU.add)),
                          reads=[rrT[2 + oc], iiT[2 + oc]] + rd, writes=[HBT])
                    if mode == "ctx":
                        P.add("pool", (lambda e, cc_=cc_: e.tensor_copy(out=call3[:, jc, DC + cc_:DC + cc_ + 1], in_=HF[:, W - 1:W])), reads=[HFT], writes=[ccT])
                        P.add("pool", (lambda e, cc_=cc_: e.tensor_copy(out=call3[:, jc, 3 * DC + cc_:3 * DC + cc_ + 1], in_=HB[:, 0:1])), reads=[HBT], writes=[ccT])
                    else:
                        P.add("pool", (lambda e: e.tensor_tensor(out=HF[:, 0:W], in0=HF[:, 0:W], in1=HB[:, 0:W], op=ALU.add)), reads=[HBT], writes=[HFT])
                        P.add("pool", (lambda e, oc=oc: e.tensor_tensor(out=hyb[:, oc, 0:W], in0=HF[:, 0:W], in1=gyb[:, oc, 0:W], op=ALU.mult)), reads=[HFT, gyT], writes=[hyT])
                if mode != "ctx":
                    store(HY[ch[0]:ch[0] + 2, :, t0:t0 + W].rearrange("m p t -> p m t"), hyb[:, :, 0:W], [hyT], [THY])

            call3 = r3(ccall, NCORES)
            store(XLp[:, :, 0:2].rearrange("m p t -> p m t"), r3(zer[:, 0:2 * DC], DC), [zT], [TXLp], slow=True)
            store(XLp[:, :, SP + 2:SP + 3].rearrange("m p t -> p m t"), r3(zer[:, 0:DC], DC), [zT], [TXLp], slow=True)
            for j in range(NCORES):
                for h in range(c.LH):
                    lru_region(h, "ctx", j * TP, TP, None, jc=j)
                    conv_some(1)
            P.add("dve", lambda e: e.memset(car, 0.0), writes=[carT])
            P.add("dve", lambda e: e.memset(chn, 0.0), writes=[chnT])
            for j in range(NCORES):
                P.add("dve", (lambda e, j=j: e.tensor_tensor(out=chn, in0=chn, in1=call3[:, j, 0:DC], op=ALU.mult)), reads=[ccT], writes=[chnT])
                P.add("dve", (lambda e, j=j: e.tensor_tensor(out=chn, in0=chn, in1=call3[:, j, DC:2 * DC], op=ALU.add)), reads=[ccT], writes=[chnT])
                P.add("dve", (lambda e, j=j: e.scalar_tensor_tensor(out=car[:, 0:DC], in0=chn, scalar=cv[:, 2 + j:3 + j], in1=car[:, 0:DC], op0=ALU.mult, op1=ALU.add)),
                      reads=[chnT, tCs], writes=[carT])
            P.add("dve", lambda e: e.memset(chn, 0.0), writes=[chnT])
            for j in range(NCORES - 1, -1, -1):
                P.add("dve", (lambda e, j=j: e.tensor_tensor(out=chn, in0=chn, in1=call3[:, j, 2 * DC:3 * DC], op=ALU.mult)), reads=[ccT], writes=[chnT])
                P.add("dve", (lambda e, j=j: e.tensor_tensor(out=chn, in0=chn, in1=call3[:, j, 3 * DC:4 * DC], op=ALU.add)), reads=[ccT], writes=[chnT])
                P.add("dve", (lambda e, j=j: e.scalar_tensor_tensor(out=car[:, DC:2 * DC], in0=chn, scalar=cv[:, 10 + j:11 + j], in1=car[:, DC:2 * DC], op0=ALU.mult, op1=ALU.add)),
                      reads=[chnT, tCs], writes=[carT])
            for h in range(c.LH):
                lru_region(h, "own", 0, TP, 0)
                lru_region(h, "sample", TP + 4, SS, TP)
            conv_some()
            P.barrier()
            sb.release(m2)

        TOM = T("OM")
        if stop not in ("s1", "s2"):
            m3 = sb.mark()
            set_wslots(128)
            SM = max(SP, SS)
            QM = max(TP, SS)
            knb = [sb.bf16(SM) for _ in range(2)]
            vb = [r3(sb.bf16(SM), SM // 128) for _ in range(2)]
            krb = sb.bf16(SM, 64)
            qnb = [sb.bf16(QM) for _ in range(2)]
            qrb = [sb.bf16(QM, 64) for _ in range(2)]
            NPT = 4
            ptb = [sb.bf16(NT) for _ in range(NPT)]
            rcp = sb.f32(NT)
            ost = [sb.bf16(NT) for _ in range(2)]
            knT = [T("kn%d" % i) for i in range(2)]
            vT = [T("v%d" % i) for i in range(2)]
            qT = [T("q%d" % i) for i in range(2)]
            krT, rcpT = T("kr"), T("rcp")
            ptT = [T("pt%d" % i) for i in range(NPT)]
            ostT = [T("ost%d" % i) for i in range(2)]
            hctr = 0
            ptc = 0
            qkc = 0
            for (KNd, KRd, Vd, tk, S_, q0, TQ) in ((KNp, KRp, Vp, TK["p"], SP, 0, TP), (KNs, KRs, Vs, TK["s"], SS, TP, SS)):
                NKT = S_ // 128
                P.add("sp", (lambda e, KRd=KRd, S_=S_: e.dma_start(out=krb[:, 0:S_], in_=KRd[:, 0:S_])), reads=[tk[1]], writes=[krT], dma=True)
                for h in range(c.MH):
                    hb_ = hctr % 2
                    hctr += 1
                    P.add("sp", (lambda e, hb_=hb_, KNd=KNd, h=h, S_=S_: e.dma_start(out=knb[hb_][:, 0:S_], in_=KNd[h, :, 0:S_])), reads=[tk[0]], writes=[knT[hb_]], dma=True)
                    P.add("sp", (lambda e, hb_=hb_, Vd=Vd, h=h, S_=S_, NKT=NKT: e.dma_start(out=vb[hb_][:, 0:NKT, :], in_=Vd[0:S_, h * 128:(h + 1) * 128].rearrange("(k p) d -> p k d", p=128))),
                          reads=[tk[2]], writes=[vT[hb_]], dma=True)
                    P.add("sp", (lambda e, hb_=hb_, h=h, q0=q0, TQ=TQ: e.dma_start(out=qnb[hb_][:, 0:TQ], in_=QN[h, :, q0:q0 + TQ])), reads=[TQN], writes=[qT[hb_]], dma=True)
                    P.add("sp", (lambda e, hb_=hb_, h=h, q0=q0, TQ=TQ: e.dma_start(out=qrb[hb_][:, 0:TQ], in_=QR[h, :, q0:q0 + TQ])), reads=[TQR], writes=[qT[hb_]], dma=True)
                    for qt in range(TQ // NT):
                        bo, bd = 3 + (qt % 2), 5 + (qt % 2)
                        qs = slice(qt * NT, (qt + 1) * NT)

                        def qk(kt, bank, hb_=hb_, qs=qs):
                            def f(e):
                                e.matmul(psb[bank], lhsT=knb[hb_][:, kt * 128:(kt + 1) * 128], rhs=qnb[hb_][:, qs], start=True, stop=False)
                                return e.matmul(psb[bank], lhsT=krb[:, kt * 128:(kt + 1) * 128], rhs=qrb[hb_][:, qs], start=False, stop=True)
                            return f
                        banks = {}
                        pts = {}

                        def issue_qk(kt):
                            nonlocal qkc
                            bank = qkc % 3
                            qkc += 1
                            banks[kt] = bank
                            P.add("pe", qk(kt, bank), reads=[knT[hb_], krT, qT[hb_]], writes=[PS[bank]])

                        def issue_exp(kt):
                            nonlocal ptc
                            pi = ptc % NPT
                            ptc += 1
                            pts[kt] = pi
                            bank = banks[kt]
                            P.add("act", (lambda e, pi=pi, bank=bank: e.activation(out=ptb[pi], in_=psb[bank], func=AF.Exp)), reads=[PS[bank]], writes=[ptT[pi]])

                        def issue_pv(kt):
                            pi = pts[kt]
                            last = (kt == NKT - 1)
                            P.add("pe", (lambda e, kt=kt, pi=pi, last=last, bo=bo, hb_=hb_: e.matmul(psb[bo], lhsT=vb[hb_][:, kt, :], rhs=ptb[pi], start=(kt == 0), stop=last)),
                                  reads=[vT[hb_], ptT[pi]], writes=[PS[bo]], noinc=not last)
                            P.add("pe", (lambda e, kt=kt, pi=pi, last=last, bd=bd: e.matmul(psb[bd], lhsT=ones, rhs=ptb[pi], start=(kt == 0), stop=last)),
                                  reads=[tO, ptT[pi]], writes=[PS[bd]], noinc=not last)
                        issue_qk(0)
                        for kt in range(NKT):
                            if kt + 1 < NKT:
                                issue_qk(kt + 1)
                            issue_exp(kt)
                            issue_pv(kt)
                        oi = qt % 2
                        P.add("dve", (lambda e, bd=bd: e.reciprocal(out=rcp, in_=psb[bd])), reads=[PS[bd]], writes=[rcpT])
                        P.add("dve", (lambda e, bo=bo, oi=oi: e.tensor_tensor(out=ost[oi], in0=psb[bo], in1=rcp, op=ALU.mult)), reads=[PS[bo], rcpT], writes=[ostT[oi]])
                        store(OM[h, :, q0 + qt * NT:q0 + (qt + 1) * NT], ost[oi], [ostT[oi]], [TOM])
            P.barrier()
            sb.release(m3)

        THR = T("HR")
        if stop not in ("s1", "s2", "s3"):
            m4 = sb.mark()
            set_wslots(WSLOT_ELEMS)
            hyt = r3(sb.bf16(DC * NT), DC)
            omt = r3(sb.bf16(c.MC * NT), c.MC)
            mgt = r3(sb.bf16(DC * NT), DC)
            ggt = [r3(sb.bf16(2 * NT), 2) for _ in range(2)]
            ta = [sb.f32(NT) for _ in range(2)]
            tb = [sb.f32(NT) for _ in range(2)]
            xr = [r3(sb.f32(4 * NT), 4) for _ in range(2)]
            hytT, omtT, mgtT = T("hyt"), T("omt"), T("mgt")
            ggT = [T("ggt%d" % i) for i in range(2)]
            tabT = [T("tab%d" % i) for i in range(2)]
            xrT = [T("xr%d" % i) for i in range(2)]
            sl, sm, so = specs["lru"], specs["mla"], specs["out"]
            for tt in range(T_ // NT):
                tc0 = tt * NT
                P.add("sp", (lambda e, tc0=tc0: e.dma_start(out=hyt, in_=HY[:, :, tc0:tc0 + NT].rearrange("m p t -> p m t"))), reads=[THY], writes=[hytT], dma=True)
                P.add("sp", (lambda e, tc0=tc0: e.dma_start(out=omt, in_=OM[:, :, tc0:tc0 + NT].rearrange("m p t -> p m t"))), reads=[TOM], writes=[omtT], dma=True)
                nbl = sl.MWt // 128
                nbm = sm.MWt // 128
                for m in range(DC):
                    if m % nbl == 0:
                        wtl, wvl = wload("lru", 0, m // nbl)
                    if m % nbm == 0:
                        wtm, wvm = wload("mla", 0, m // nbm)
                    gi = m % 2
                    P.add("sp", (lambda e, gi=gi, m=m, tc0=tc0: e.dma_start(out=ggt[gi], in_=GG[m:m + DC + 1:DC, :, tc0:tc0 + NT].rearrange("m p t -> p m t"))),
                          reads=[TGG], writes=[ggT[gi]], dma=True)
                    b1, b2 = next_pm(), next_pm()
                    col, com = (m % nbl) * 128, (m % nbm) * 128
                    P.add("pe", mm_group(lambda k, wvl=wvl, col=col: wvl[:, k, col:col + 128], lambda k: hyt[:, k, :], DC, b1), reads=[wtl, hytT], writes=[PS[b1]])
                    P.add("pe", mm_group(lambda k, wvm=wvm, com=com: wvm[:, k, com:com + 128], lambda k: omt[:, k, :], c.MC, b2), reads=[wtm, omtT], writes=[PS[b2]])
                    P.add("dve", (lambda e, gi=gi, b1=b1: e.tensor_tensor(out=ta[gi], in0=psb[b1], in1=ggt[gi][:, 0, :], op=ALU.mult)), reads=[PS[b1], ggT[gi]], writes=[tabT[gi]])
                    P.add("dve", (lambda e, gi=gi, b2=b2: e.tensor_tensor(out=tb[gi], in0=psb[b2], in1=ggt[gi][:, 1, :], op=ALU.mult)), reads=[PS[b2], ggT[gi]], writes=[tabT[gi]])
                    P.add("pool", (lambda e, gi=gi, m=m: e.tensor_tensor(out=mgt[:, m, :], in0=ta[gi], in1=tb[gi], op=ALU.add)), reads=[tabT[gi]], writes=[mgtT])
                for n in range(D // 512):
                    xi = n % 2
                    P.add("sp", (lambda e, xi=xi, n=n, tc0=tc0: e.dma_start(out=xr[xi], in_=xown[tc0:tc0 + NT, n * 512:(n + 1) * 512].rearrange("(s p) n -> p s n", p=128))),
                          writes=[xrT[xi]], dma=True)
                    wts = [wload("out", kg, n) for kg in range(so.KG)]
                    for s_ in range(4):
                        bank = next_pm()

                        def f(e, s_=s_, bank=bank, wts=wts):
                            ins = None
                            for k in range(DC):
                                wv = wts[k // so.KCt][1]
                                ins = e.matmul(psb[bank], lhsT=mgt[:, k, s_ * 128:(s_ + 1) * 128], rhs=wv[:, k % so.KCt, :], start=(k == 0), stop=(k == DC - 1))
                            return ins
                        P.add("pe", f, reads=[mgtT] + [w[0] for w in wts], writes=[PS[bank]])
                        P.add("dve", (lambda e, xi=xi, s_=s_, bank=bank: e.tensor_tensor(out=xr[xi][:, s_, :], in0=psb[bank], in1=xr[xi][:, s_, :], op=ALU.add)),
                              reads=[PS[bank]], writes=[xrT[xi]])
                    store(HR[tc0:tc0 + NT, n * 512:(n + 1) * 512].rearrange("(s p) n -> p s n", p=128), xr[xi], [xrT[xi]], [THR])
            P.barrier()
            sb.release(m4)

            m5 = sb.mark()
            set_wslots(WSLOT_ELEMS)
            hres = r3(sb.f32(4 * D), 4)
            n2b = sb.bf16(D)
            n2T = r3(sb.bf16(DC * NT), DC)
            FP = 8 if c.FC >= 64 else (4 if c.FC >= 4 else 1)
            FCP = c.FC // FP
            uT = r3(sb.bf16(FCP * NT), FCP)
            rl = [sb.f32(NT) for _ in range(2)]
            nfb = sb.f32(D)
            sm2 = sb.f32(16)
            hresT, n2bT, n2TT, uTT, nfT = T("hres"), T("n2b"), T("n2T"), T("uT"), T("nfb")
            rlT = [T("rl%d" % i) for i in range(2)]
            sm2T = [T("sm2_%d" % i, small=True) for i in range(4)]
            TY = T("Y")
            P.add("sp", lambda e: e.dma_start(out=nfb, in_=normf.partition_broadcast(128)), writes=[nfT], dma=True)
            su, sd = specs["up"], specs["down"]
            nbu = su.MWt // 128
            for tt in range(T_ // NT):
                tc0 = tt * NT
                P.add("sp", (lambda e, tc0=tc0: e.dma_start(out=hres, in_=HR[tc0:tc0 + NT, :].rearrange("(s p) n -> p s n", p=128))), reads=[THR], writes=[hresT], dma=True)
                for s_ in range(4):
                    P.add("act", (lambda e, s_=s_: e.activation(out=n2b, in_=hres[:, s_, :], func=AF.Square, accum_out=sm2[:, 0:1])), reads=[hresT], writes=[n2bT, sm2T[0]])
                    P.add("act", (lambda e: e.activation(out=sm2[:, 1:2], in_=sm2[:, 0:1], func=AF.Sqrt, scale=1.0 / D, bias=EPS)), reads=[sm2T[0]], writes=[sm2T[1]])
                    P.add("dve", (lambda e: e.reciprocal(out=sm2[:, 1:2], in_=sm2[:, 1:2])), reads=[sm2T[1]], writes=[sm2T[1]])
                    P.add("act", (lambda e, s_=s_: e.activation(out=n2b, in_=hres[:, s_, :], func=AF.Copy, scale=sm2[:, 1:2])), reads=[hresT, sm2T[1]], writes=[n2bT])
                    for g in range(0, DC, 8):
                        ng = min(8, DC - g)
                        bank = (g // 8) % 2
                        pbf = psb[bank].bitcast(BF16)

                        def tr(e, g=g, ng=ng, pbf=pbf):
                            ins = None
                            for j in range(ng):
                                ins = e.transpose(pbf[:, j * 128:(j + 1) * 128], n2b[:, (g + j) * 128:(g + j + 1) * 128], ident)
                            return ins
                        P.add("pe", tr, reads=[n2bT, tC], writes=[PS[bank]])
                        P.add("dve", (lambda e, g=g, ng=ng, pbf=pbf, s_=s_: e.tensor_tensor(out=n2T[:, g:g + ng, s_ * 128:(s_ + 1) * 128], in0=r3(pbf[:, 0:ng * 128], ng),
                                                                                         in1=pv_norm2[:, g:g + ng].unsqueeze(2).to_broadcast([128, ng, 128]), op=ALU.mult)),
                              reads=[PS[bank], tCs], writes=[n2TT])
                for fp in range(FP):
                    for mb in range(FCP):
                        m = fp * FCP + mb
                        if m % nbu == 0:
                            wt, wv = wload("up", 0, m // nbu)
                        co = (m % nbu) * 128
                        bank = next_pm()
                        P.add("pe", mm_group(lambda k, wv=wv, co=co: wv[:, k, co:co + 128], lambda k: n2T[:, k, :], DC, bank), reads=[wt, n2TT], writes=[PS[bank]])
                        ri = m % 2
                        P.add("act", (lambda e, ri=ri, bank=bank: e.activation(out=rl[ri], in_=psb[bank], func=AF.Relu)), reads=[PS[bank]], writes=[rlT[ri]])
                        P.add("pool", (lambda e, ri=ri, mb=mb: e.tensor_tensor(out=uT[:, mb, :], in0=rl[ri], in1=rl[ri], op=ALU.mult)), reads=[rlT[ri]], writes=[uTT])
                    kg0 = fp * FCP // sd.KCt
                    kg1 = (fp * FCP + FCP - 1) // sd.KCt
                    for n in range(D // 512):
                        wts = [wload("down", kg, n) for kg in range(kg0, kg1 + 1)]
                        for s_ in range(4):
                            bank = next_pm()

                            def f(e, s_=s_, bank=bank, wts=wts, fp=fp, kg0=kg0):
                                ins = None
                                for k in range(FCP):
                                    kgl = fp * FCP + k
                                    wv = wts[kgl // sd.KCt - kg0][1]
                                    ins = e.matmul(psb[bank], lhsT=uT[:, k, s_ * 128:(s_ + 1) * 128], rhs=wv[:, kgl % sd.KCt, :], start=(k == 0), stop=(k == FCP - 1))
                                return ins
                            P.add("pe", f, reads=[uTT] + [w[0] for w in wts], writes=[PS[bank]])
                            P.add("dve", (lambda e, s_=s_, n=n, bank=bank: e.tensor_tensor(out=hres[:, s_, n * 512:(n + 1) * 512], in0=psb[bank], in1=hres[:, s_, n * 512:(n + 1) * 512], op=ALU.add)),
                                  reads=[PS[bank]], writes=[hresT])
                for s_ in range(4):
                    P.add("act", (lambda e, s_=s_: e.activation(out=n2b, in_=hres[:, s_, :], func=AF.Square, accum_out=sm2[:, 2:3])), reads=[hresT], writes=[n2bT, sm2T[2]])
                    P.add("act", (lambda e: e.activation(out=sm2[:, 3:4], in_=sm2[:, 2:3], func=AF.Sqrt, scale=1.0 / D, bias=EPS)), reads=[sm2T[2]], writes=[sm2T[3]])
                    P.add("dve", (lambda e: e.reciprocal(out=sm2[:, 3:4], in_=sm2[:, 3:4])), reads=[sm2T[3]], writes=[sm2T[3]])
                    P.add("dve", (lambda e, s_=s_: e.scalar_tensor_tensor(out=hres[:, s_, :], in0=hres[:, s_, :], scalar=sm2[:, 3:4], in1=nfb, op0=ALU.mult, op1=ALU.mult)),
                          reads=[sm2T[3], nfT], writes=[hresT])
                store(y_out[tc0:tc0 + NT, :].rearrange("(s p) n -> p s n", p=128), hres, [hresT], [TY])
            P.barrier()
            sb.release(m5)

        P.emit()
    return nc


def host_layout(c, inp):
    f = np.float32
    D = c.D
    g = lambda k: np.asarray(inp[k], dtype=f)
    w_in = g("w_in")[0]
    o_q, o_kv, o_kr, o_gg = 2 * D, 2 * D + c.QL, 2 * D + c.QL + c.KVL, 2 * D + c.QL + c.KVL + 64
    kr = w_in[:, o_kr:o_kr + 64]
    pad = c.INC - (w_in.shape[1] + 64)
    w_in_p = np.concatenate([w_in[:, :o_kr], kr, kr[:, 32:], kr[:, :32], w_in[:, o_gg:], np.zeros((D, pad), f)], axis=1)
    wq = g("w_q_up")[0].reshape(c.QL, c.MH, 192)
    wq_p = np.concatenate([wq[:, :, :192], wq[:, :, 160:192], wq[:, :, 128:160]], axis=2).reshape(c.QL, c.MH * 256)
    wkv = g("w_kv_up")[0].reshape(c.KVL, c.MH, 256)
    wkvk = np.ascontiguousarray(wkv[:, :, :128]).reshape(c.KVL, c.MW)
    wkvv = np.ascontiguousarray(wkv[:, :, 128:]).reshape(c.KVL, c.MW)
    wa, wx = g("lru_wa")[0], g("lru_wx")[0]
    ga = np.stack([wa, wx], axis=1)
    ga = ga.transpose(3, 0, 1, 2, 4).reshape(256, 4 * c.LH * 256)
    W = {"in": w_in_p, "qup": wq_p, "kvk": wkvk, "kvv": wkvv, "ga": ga, "lru": g("w_lru_proj")[0], "mla": g("w_mla_proj")[0],
         "out": g("w_out")[0], "up": g("w_up")[0], "down": g("w_down")[0]}
    shared = {}
    for s in weight_specs(c):
        shared["w_" + s.name] = pretile(W[s.name], s.KCt, s.MWt)

    def fm(v):
        v = np.asarray(v, f).reshape(-1, 128)
        return v.T
    cw = g("conv_w")[0]
    ba, bx, lam = g("lru_ba")[0].reshape(2, D), g("lru_bx")[0].reshape(2, D), g("lru_lam")[0]
    cols = [fm(g("norm1")[0]), fm(g("norm2")[0])] + [fm(cw[k]) for k in range(4)] + [fm(g("conv_b")[0])] + \
           [fm(ba[0]), fm(ba[1]), fm(bx[0]), fm(bx[1]), fm(lam[0]), fm(lam[1]), fm(g("q_norm")[0]), fm(g("kv_norm")[0])]
    shared["pvec"] = np.ascontiguousarray(np.concatenate(cols, axis=1))
    shared["normf"] = g("norm_f").reshape(1, D)
    xp = g("x_prompt")[0]
    xs = g("x_sample")
    shared["xp"] = xp
    invf = (1.0 / (np.float32(10000.0) ** (np.arange(0, 64, 2, dtype=f) / np.float32(64)))).astype(f)
    maps = []
    for i in range(NCORES):
        m = dict(shared)
        m["xown"] = np.ascontiguousarray(np.concatenate([xp[i * c.TP:(i + 1) * c.TP], xs[i]], axis=0))
        halo = np.zeros((4, D), f)
        if i > 0:
            halo[0:2] = xp[i * c.TP - 2:i * c.TP]
        if i < NCORES - 1:
            halo[2] = xp[(i + 1) * c.TP]
        m["xhalo"] = halo
        cvv = np.zeros((128, 24), f)
        cvv[0:32, 0] = invf
        cvv[32:64, 0] = invf
        cvv[:, 1] = i * c.TP
        if i > 0:
            cvv[:, 2 + i - 1] = 1.0
        if i < NCORES - 1:
            cvv[:, 10 + i + 1] = 1.0
        m["cvec"] = cvv
        maps.append(m)
    return maps


_CACHE = {}


def kernel(**inputs):
    c = Cfg()
    if "nc" not in _CACHE:
        _CACHE["nc"] = build(c)
    nc = _CACHE["nc"]
    maps = host_layout(c, inputs)
    res = run_bass_kernel_spmd(nc, maps, core_ids=list(range(NCORES))).results
    yp = np.concatenate([res[i]["y"][:c.TP] for i in range(NCORES)], axis=0)[None]
    ys = np.stack([res[i]["y"][c.TP:] for i in range(NCORES)], axis=0)
    return (np.ascontiguousarray(yp, dtype=np.float32), np.ascontiguousarray(ys, dtype=np.float32))
```

```python
import math
import numpy as np
import concourse.bass as bass
import concourse.mybir as mybir
from concourse.bass_utils import run_bass_kernel_spmd

F32 = mybir.dt.float32
BF16 = mybir.dt.bfloat16
I32 = mybir.dt.int32
ALU = mybir.AluOpType
AF = mybir.ActivationFunctionType
AX = mybir.AxisListType

NCORES = 8
EPS = 1e-6
NT = 512
NSLOT = 8


class Cfg:
    def __init__(self, D=4096, LH=16, MH=32, QL=1024, KVL=512, DFF=16384, SP=8192, SS=2048):
        self.D, self.LH, self.MH, self.QL, self.KVL, self.DFF, self.SP, self.SS = D, LH, MH, QL, KVL, DFF, SP, SS
        self.DC = D // 128
        self.QC = QL // 128
        self.KVC = KVL // 128
        self.FC = DFF // 128
        self.MW = MH * 128
        self.MC = self.MW // 128
        self.TP = SP // NCORES
        self.T = self.TP + SS
        self.INC = (2 * D + QL + KVL + 128 + 2 * D + 255) // 256 * 256
        assert self.TP % NT == 0 and SS % NT == 0 and D % 256 == 0


class T:
    __slots__ = ("name", "ws", "rs", "small", "excl")

    def __init__(self, name, small=False, excl=False):
        self.name, self.ws, self.rs, self.small, self.excl = name, {}, {}, small, excl


class Op:
    __slots__ = ("eng", "fn", "deps", "dma", "slot", "milestone", "count", "noinc", "seq")


ENGS = ("pe", "act", "dve", "pool", "sp")


class Prog:
    def __init__(self, nc):
        self.nc = nc
        self.ops = {e: [] for e in ENGS}
        self.dma_next = {e: 0 for e in ENGS}
        self.dma_cnt = {}
        self.seq = 0

    def add(self, eng, fn, reads=(), writes=(), dma=False, noinc=False):
        ops = self.ops[eng]
        idx = len(ops)
        deps = {}
        if any(t.excl for t in reads):
            writes = list(writes) + [t for t in reads if t.excl and t not in writes]
            reads = [t for t in reads if not t.excl]

        def merge(d, small):
            for k, v in d.items():
                if k == ("c", eng) and eng == "pe":
                    continue
                if deps.get(k, -1) < v:
                    deps[k] = v

        for t in reads:
            merge(t.ws, t.small)
        for t in writes:
            merge(t.ws, t.small)
            merge(t.rs, t.small)
        op = Op()
        op.eng, op.fn, op.deps, op.dma, op.slot, op.milestone, op.count = eng, fn, deps, dma, None, False, 0
        op.noinc, op.seq = noinc, self.seq
        self.seq += 1
        if dma:
            if dma == "cc":
                qn, slot = "cc", 0
            else:
                qn, slot = eng, self.dma_next[eng] % NSLOT
                self.dma_next[eng] += 1
            val = self.dma_cnt.get((qn, slot), 0) + 16
            self.dma_cnt[(qn, slot)] = val
            key = ("d", qn, slot)
            if val > 16:
                deps[key] = max(deps.get(key, 0), val - 16)
            op.slot = (qn, slot)
        else:
            key, val = ("c", eng), idx
        ops.append(op)
        for t in reads:
            if t.rs.get(key, -1) < val:
                t.rs[key] = val
        for t in writes:
            if t.rs:
                t.ws = {key: val}
                t.rs = {}
            elif t.ws.get(key, -1) < val:
                t.ws[key] = val
        return op

    def barrier(self):
        last = {}
        for e in ENGS:
            for i in range(len(self.ops[e]) - 1, -1, -1):
                if not self.ops[e][i].dma:
                    last[("c", e)] = i
                    break
        for (e, s), v in self.dma_cnt.items():
            last[("d", e, s)] = v
        for e in ENGS:
            deps = {k: v for k, v in last.items() if k != ("c", e)}
            op = Op()
            op.eng, op.fn, op.deps, op.dma, op.slot, op.milestone, op.count = e, (lambda en: en.nop()), deps, False, None, False, 0
            op.noinc, op.seq = False, self.seq
            self.seq += 1
            self.ops[e].append(op)

    def emit(self):
        nc = self.nc
        redir = {}
        pe_ops = self.ops["pe"]
        nxt = None
        for i in range(len(pe_ops) - 1, -1, -1):
            if not pe_ops[i].noinc:
                nxt = i
            redir[i] = nxt
        for e in ENGS:
            for op in self.ops[e]:
                for k, v in list(op.deps.items()):
                    if k == ("c", "pe") and redir[v] != v:
                        v2 = redir[v]
                        assert v2 is not None and pe_ops[v2].seq < op.seq, ("open-group milestone cannot be redirected", v, v2)
                        op.deps[k] = v2
        for e in ENGS:
            for op in self.ops[e]:
                for k, v in op.deps.items():
                    if k[0] == "c":
                        self.ops[k[1]][v].milestone = True
        for e in ENGS:
            c = 0
            for op in self.ops[e]:
                if op.milestone:
                    c += 1
                op.count = c
        csem = {e: nc.alloc_semaphore(name="c_" + e) for e in ENGS}
        dsem = {k: nc.alloc_semaphore(name="d_%s_%d" % k) for k in self.dma_cnt}
        allops = self.ops
        dma_cnt = self.dma_cnt

        def stream(e_name, en):
            waited = {}
            for op in allops[e_name]:
                for k, v in op.deps.items():
                    if k[0] == "c":
                        sem, val = csem[k[1]], allops[k[1]][v].count
                    else:
                        sem, val = dsem[(k[1], k[2])], v
                    if waited.get(k, 0) >= val:
                        continue
                    en.wait_ge(sem, val)
                    waited[k] = val
                ins = op.fn(en)
                if op.dma:
                    ins.then_inc(dsem[op.slot], 16)
                elif op.milestone:
                    ins.then_inc(csem[e_name], 1)
            for (e2, s), v in dma_cnt.items():
                if (e2 == e_name or (e2 == "cc" and e_name == "pool")) and waited.get(("d", e2, s), 0) < v:
                    en.wait_ge(dsem[(e2, s)], v)

        with nc.Block() as block:
            @block.tensor
            def _(en):
                stream("pe", en)

            @block.scalar
            def _(en):
                stream("act", en)

            @block.vector
            def _(en):
                stream("dve", en)

            @block.gpsimd
            def _(en):
                stream("pool", en)

            @block.sync
            def _(en):
                stream("sp", en)


class SB:
    def __init__(self, ap, nwords):
        self.ap, self.n, self.off = ap, nwords, 0

    def mark(self):
        return self.off

    def release(self, m):
        self.off = m

    def f32(self, n, parts=128):
        o = self.off
        self.off += n
        assert self.off <= self.n, ("SBUF overflow", self.off, self.n)
        return self.ap[0:parts, o:o + n]

    def bf16(self, n, parts=128):
        w = (n + 1) // 2
        o = self.off
        self.off += w
        assert self.off <= self.n, ("SBUF overflow", self.off, self.n)
        return self.ap[0:parts, o:o + w].bitcast(BF16)[:, 0:n]

    def i32(self, n, parts=128):
        return self.f32(n, parts).bitcast(I32)


def r3(ap, a):
    return ap.rearrange("p (a b) -> p a b", a=a)


def pretile(W, KCt, MWt):
    K, N = W.shape
    KG, NG = K // (KCt * 128), N // MWt
    assert KG * KCt * 128 == K and NG * MWt == N, (W.shape, KCt, MWt)
    return np.ascontiguousarray(
        W.reshape(KG, KCt, 128, NG, MWt).transpose(0, 3, 2, 1, 4)).reshape(KG * NG * 128, KCt * MWt)


class WSpec:
    def __init__(self, name, K, N, KCt, MWt):
        self.name, self.K, self.N, self.KCt, self.MWt = name, K, N, KCt, MWt
        self.KG, self.NG = K // (KCt * 128), N // MWt
        self.rows, self.cols = self.KG * self.NG * 128, KCt * MWt


def weight_specs(c):
    kq = c.QC
    kv = c.KVC
    return [
        WSpec("in", c.D, c.INC, c.DC, 16384 // (c.DC * 2) if c.DC >= 32 else 128),
        WSpec("qup", c.QL, c.MH * 256, kq, min(8192 // kq, c.MH * 256)),
        WSpec("kvk", c.KVL, c.MW, kv, min(8192 // kv, c.MW)),
        WSpec("kvv", c.KVL, c.MW, kv, min(8192 // kv, c.MW)),
        WSpec("ga", 256, 4 * c.LH * 256, 2, 1024),
        WSpec("lru", c.D, c.D, c.DC, 16384 // (c.DC * 2) if c.DC >= 32 else 128),
        WSpec("mla", c.MW, c.D, c.MC, 16384 // (c.MC * 2) if c.MC >= 32 else 128),
        WSpec("out", c.D, c.D, min(c.DC, 16), 512),
        WSpec("up", c.D, c.DFF, c.DC, 16384 // (c.DC * 2) if c.DC >= 32 else 128),
        WSpec("down", c.DFF, c.D, min(16, c.FC), 512),
    ]


WSLOT_ELEMS = 8192


def build(c, debug=(), stop=None):
    nc = bass.Bass("TRN2", target_bir_lowering=False)
    P = Prog(nc)
    D, DC, T_, TP, SS, SP = c.D, c.DC, c.T, c.TP, c.SS, c.SP
    specs = {s.name: s for s in weight_specs(c)}
    for s in specs.values():
        assert s.cols <= WSLOT_ELEMS, (s.name, s.cols)

    def din(name, shape, dt=F32):
        return nc.dram_tensor(name, list(shape), dt, kind="ExternalInput").ap()

    def dscr(name, shape, dt):
        kind = "ExternalOutput" if name in debug else "Internal"
        return nc.dram_tensor(name, list(shape), dt, kind=kind).ap()

    xown = din("xown", [T_, D])
    xp = din("xp", [SP, D])
    xhalo = din("xhalo", [4, D])
    NPV = 2 * DC + 4 * DC + DC + 6 * DC + c.QC + c.KVC
    pvec = din("pvec", [128, NPV])
    cvec = din("cvec", [128, 24])
    normf = din("normf", [1, D])
    wf = {n: din("w_" + n, [s.rows, s.cols]) for n, s in specs.items()}
    wb = {n: dscr("wb_" + n, [s.rows, s.cols], BF16) for n, s in specs.items()}
    y_out = nc.dram_tensor("y", [T_, D], F32, kind="ExternalOutput").ap()

    XLW = T_ + 8
    XL = dscr("XL", [DC, 128, XLW], F32)
    XLp = dscr("XLp", [DC, 128, SP + 4], F32)
    GY = dscr("GY", [DC, 128, T_], BF16)
    GG = dscr("GG", [2 * DC, 128, T_], BF16)
    QN = dscr("QN", [c.MH, 128, T_], BF16)
    QR = dscr("QR", [c.MH, 64, T_], BF16)
    KNs = dscr("KNs", [c.MH, 128, SS], BF16)
    KRs = dscr("KRs", [64, SS], BF16)
    Vs = dscr("Vs", [SS, c.MW], BF16)
    KNp = dscr("KNp", [c.MH, 128, SP], BF16)
    KRp = dscr("KRp", [64, SP], BF16)
    Vp = dscr("Vp", [SP, c.MW], BF16)
    HY = dscr("HY", [DC, 128, T_], BF16)
    OM = dscr("OM", [c.MC, 128, T_], BF16)
    HR = dscr("HR", [T_, D], F32)

    SBW = 53200
    with (nc.sbuf_tensor("sb", [128, SBW], F32) as sbt, nc.psum_tensor("ps", [128, 8, 512], F32) as pst):
        sb = SB(sbt, SBW)
        PS = [T("ps%d" % i, excl=True) for i in range(8)]
        psb = [pst[:, i, :] for i in range(8)]

        ident_f = sb.f32(128)
        ident = sb.bf16(128)
        ones = sb.bf16(128)
        pv = sb.f32(NPV)
        cv = sb.f32(24)
        iota_t = sb.f32(NT, 64)
        rsc = sb.f32(4, 64)
        tC = T("consts")
        tCs = T("consts_small", small=True)
        o = 0
        pv_norm1 = pv[:, o:o + DC]; o += DC
        pv_norm2 = pv[:, o:o + DC]; o += DC
        pv_convw = [pv[:, o + k * DC:o + (k + 1) * DC] for k in range(4)]; o += 4 * DC
        pv_convb = pv[:, o:o + DC]; o += DC
        pv_ba = [pv[:, o + k * DC:o + (k + 1) * DC] for k in range(2)]; o += 2 * DC
        pv_bx = [pv[:, o + k * DC:o + (k + 1) * DC] for k in range(2)]; o += 2 * DC
        pv_lam = [pv[:, o + k * DC:o + (k + 1) * DC] for k in range(2)]; o += 2 * DC
        pv_qn = pv[:, o:o + c.QC]; o += c.QC
        pv_kvn = pv[:, o:o + c.KVC]; o += c.KVC
        assert o == NPV

        P.add("sp", lambda e: e.dma_start(out=pv, in_=pvec[:, :]), writes=[tCs], dma=True)
        P.add("sp", lambda e: e.dma_start(out=cv, in_=cvec[:, :]), writes=[tCs], dma=True)

        tI = T("ident_f")
        P.add("pool", lambda e: e.memset(ident_f, 0.0), writes=[tI])
        P.add("pool", lambda e: e.affine_select(out=ident_f, in_=ident_f, pattern=[[-1, 128]], compare_op=ALU.not_equal,
                                                fill=1.0, base=0, channel_multiplier=1), reads=[tI], writes=[tI])
        P.add("pool", lambda e: e.iota(iota_t, pattern=[[1, NT]], base=0, channel_multiplier=0,
                                       allow_small_or_imprecise_dtypes=True), writes=[tC])
        P.add("dve", lambda e: e.tensor_copy(out=ident, in_=ident_f), reads=[tI], writes=[tC])
        tO = T("ones")
        P.add("dve", lambda e: e.memset(ones, 1.0), writes=[tO])
        TWO_PI = 2.0 * math.pi

        tR0, tR1 = T("rsc0"), T("rsc1")
        P.add("dve", lambda e: e.tensor_scalar(out=rsc[:, 0:1], in0=cv[0:64, 0:1], scalar1=1.0 / TWO_PI, scalar2=None, op0=ALU.mult),
              reads=[tCs], writes=[tR0])
        P.add("dve", lambda e: e.memset(rsc[0:32, 1:2], -TWO_PI * (1.0 - 2e-6)), writes=[tR1])
        P.add("dve", lambda e: e.memset(rsc[32:64, 1:2], TWO_PI * (1.0 - 2e-6)), writes=[tR1])

        TW = {n: T("wb_" + n) for n in specs}
        order = ["in", "qup", "kvk", "kvv", "ga", "lru", "mla", "out", "up", "down"]
        conv_pending = []
        for n in order:
            s = specs[n]
            rows_per = max(128, (8 * 1024 * 1024 // (s.cols * 4)) // 128 * 128)
            r = 0
            while r < s.rows:
                r2 = min(s.rows, r + rows_per)
                conv_pending.append((n, r, r2))
                r = r2

        def conv_some(k=None, upto=None):
            while conv_pending and (k is None or k > 0):
                if upto is not None and conv_pending[0][0] not in upto:
                    break
                n, r, r2 = conv_pending.pop(0)
                P.add("pool", (lambda e, n=n, r=r, r2=r2: e.dma_start(out=wb[n][r:r2, :], in_=wf[n][r:r2, :])),
                      writes=[TW[n]], dma=True)
                if k is not None:
                    k -= 1
        conv_some(upto=("in", "qup", "kvk", "kvv", "ga"))

        NWS = 3
        wslots = []
        wT = []
        wctr = [0]

        def set_wslots(nelems):
            wslots[:] = [sb.bf16(nelems) for _ in range(NWS)]
            wT[:] = [T("wslot%d_%d" % (i, wctr[0])) for i in range(NWS)]

        def wload(name, kg, ng):
            s = specs[name]
            i = wctr[0] % NWS
            wctr[0] += 1
            t = kg * s.NG + ng
            view = wslots[i][:, 0:s.cols]
            P.add("sp", (lambda e, name=name, t=t, view=view: e.dma_start(out=view, in_=wb[name][t * 128:(t + 1) * 128, :])),
                  reads=[TW[name]], writes=[wT[i]], dma=True)
            return wT[i], r3(view, s.KCt)

        m1 = sb.mark()
        set_wslots(WSLOT_ELEMS)
        xbuf = [sb.f32(D) for _ in range(1)]
        xT = [T("xbuf%d" % i) for i in range(1)]
        nbuf = sb.bf16(D)
        nTt = T("nbuf")
        nT = [r3(sb.bf16(DC * NT), DC) for _ in range(2)]
        nTT = [T("nT%d" % i) for i in range(2)]
        small = sb.f32(64)
        smallT = [T("small%d" % i, small=True) for i in range(16)]
        stg = [sb.f32(4 * NT) for _ in range(3)]
        stgT = [T("stg%d" % i) for i in range(3)]
        stgc = [0]
        cq = r3(sb.bf16(c.QC * NT), c.QC)
        ckv = r3(sb.bf16(c.KVC * NT), c.KVC)
        sq = sb.bf16(NT)
        cqT, ckvT, sqT = T("cq"), T("ckv"), T("sq")
        serT = T("ser")
        rstdq = sb.f32(NT)
        rstdkv = sb.f32(NT)
        rstdtok = sb.f32(4)
        rqT, rkT, rtT = T("rstdq"), T("rstdkv"), T("rstdtok", small=True)
        cos2 = sb.f32(NT, 64)
        sins = sb.f32(NT, 64)
        cosq = sb.f32(NT, 64)
        sinq = sb.f32(NT, 64)
        ropeT, ropeqT = T("rope"), T("ropeq")
        ru = sb.f32(NT, 64)
        rk = sb.i32(NT, 64)
        rkf = sb.f32(NT, 64)
        rtmpT = T("ropetmp")
        t1 = [sb.f32(NT) for _ in range(2)]
        t2 = [sb.f32(NT) for _ in range(2)]
        t12T = [T("t12_%d" % i) for i in range(2)]
        SCALE = 192.0 ** -0.5

        pm_ctr = [0]
        cvc = [0]

        def next_pm():
            i = pm_ctr[0] % 4
            pm_ctr[0] += 1
            return 2 + i

        def get_stg():
            i = stgc[0] % 3
            stgc[0] += 1
            return i

        def prep_tile(src, row0, slot, tile_idx):
            for s_ in range(NT // 128):
                xi = 0
                r0 = row0 + s_ * 128
                P.add("sp", (lambda e, xi=xi, r0=r0: e.dma_start(out=xbuf[xi], in_=src[r0:r0 + 128, :])),
                      writes=[xT[xi]], dma=True)
                ssq = small[:, 0:1]
                rst = small[:, 1:2]
                P.add("act", (lambda e, xi=xi: e.activation(out=nbuf, in_=xbuf[xi], func=AF.Square, accum_out=ssq)),
                      reads=[xT[xi]], writes=[nTt, smallT[0]])
                P.add("act", (lambda e: e.activation(out=rst, in_=ssq, func=AF.Sqrt, scale=1.0 / D, bias=EPS)),
                      reads=[smallT[0]], writes=[smallT[1]])
                P.add("dve", (lambda e: e.reciprocal(out=rst, in_=rst)), reads=[smallT[1]], writes=[smallT[1]])
                P.add("act", (lambda e, xi=xi: e.activation(out=nbuf, in_=xbuf[xi], func=AF.Copy, scale=rst)),
                      reads=[xT[xi], smallT[1]], writes=[nTt])
                for g in range(0, DC, 8):
                    ng = min(8, DC - g)
                    bank = (g // 8) % 2
                    pbf = psb[bank].bitcast(BF16)

                    def tr(e, g=g, ng=ng, pbf=pbf):
                        ins = None
                        for j in range(ng):
                            ins = e.transpose(pbf[:, j * 128:(j + 1) * 128], nbuf[:, (g + j) * 128:(g + j + 1) * 128], ident)
                        return ins
                    P.add("pe", tr, reads=[nTt, tC], writes=[PS[bank]])

                    def ev(e, g=g, ng=ng, pbf=pbf, s_=s_):
                        return e.tensor_tensor(
                            out=nT[slot][:, g:g + ng, s_ * 128:(s_ + 1) * 128],
                            in0=r3(pbf[:, 0:ng * 128], ng),
                            in1=pv_norm1[:, g:g + ng].unsqueeze(2).to_broadcast([128, ng, 128]),
                            op=ALU.mult)
                    P.add("dve", ev, reads=[PS[bank], tCs], writes=[nTT[slot]])

        def rope_tables(p0, posoff_col=None):
            if posoff_col is None:
                P.add("dve", lambda e: e.tensor_scalar(out=ru, in0=iota_t, scalar1=float(p0), scalar2=rsc[:, 0:1], op0=ALU.add, op1=ALU.mult),
                      reads=[tC, tR0], writes=[rtmpT])
            else:
                P.add("dve", lambda e: e.tensor_scalar(out=ru, in0=iota_t, scalar1=cv[0:64, posoff_col:posoff_col + 1], scalar2=float(p0),
                                                       op0=ALU.add, op1=ALU.add), reads=[tC, tCs], writes=[rtmpT])
                P.add("dve", lambda e: e.tensor_scalar(out=ru, in0=ru, scalar1=rsc[:, 0:1], scalar2=None, op0=ALU.mult),
                      reads=[rtmpT, tR0], writes=[rtmpT])
            for which in range(2):
                if which == 1:
                    P.add("dve", lambda e: e.tensor_scalar(out=ru, in0=ru, scalar1=0.25, scalar2=None, op0=ALU.add), reads=[rtmpT], writes=[rtmpT])
                P.add("dve", lambda e: e.tensor_copy(out=rk, in_=ru), reads=[rtmpT], writes=[rtmpT])
                P.add("dve", lambda e: e.tensor_copy(out=rkf, in_=rk), reads=[rtmpT], writes=[rtmpT])
                P.add("dve", lambda e: e.tensor_tensor(out=rkf, in0=ru, in1=rkf, op=ALU.subtract), reads=[rtmpT], writes=[rtmpT])
                if which == 0:
                    P.add("act", lambda e: e.activation(out=sins, in_=rkf, func=AF.Sin, scale=rsc[:, 1:2]),
                          reads=[rtmpT, tR1], writes=[ropeT])
                else:
                    P.add("act", lambda e: e.activation(out=cos2, in_=rkf, func=AF.Sin, scale=TWO_PI * (1.0 - 2e-6)),
                          reads=[rtmpT], writes=[ropeT])

        def mm_group(lhs_fn, rhs_fn, KC, bank, M=128, N=NT, extra_reads=()):
            def f(e):
                ins = None
                for k in range(KC):
                    ins = e.matmul(psb[bank][0:M, 0:N], lhsT=lhs_fn(k), rhs=rhs_fn(k), start=(k == 0), stop=(k == KC - 1))
                return ins
            return f

        def store(dst_ap, src_ap, reads, writes, slow=False):
            P.add("pool", (lambda e: e.dma_start(out=dst_ap, in_=src_ap, allow_slow_non_contiguous=slow)), reads=reads, writes=writes, dma=True)

        TXL, TGY, TGG, TQN, TQR = T("XL"), T("GY"), T("GG"), T("QN"), T("QR")
        TXLp = T("XLp")
        TK = {"s": (T("KNs"), T("KRs"), T("Vs")), "p": (T("KNp"), T("KRp"), T("Vp"))}

        def in_proj_tile(slot, kind, tcol, xlcol, pos0, posoff_col, kvdst, kvcol, prep_next=None):
            s_in = specs["in"]
            MWt = s_in.MWt
            nblk_tile = MWt // 128
            import os
            do_main = kind in ("prompt", "sample")
            do_kv = kind in ("ctx", "sample")
            do_q = do_main
            dq = do_q and os.environ.get("KDBG", "") != "noq"
            if os.environ.get("KDBG", "") == "nomain":
                do_main = False
            rope_tables(pos0, posoff_col)
            b_xl, b_gy, b_cq, b_ckv, b_kr, b_gg = 0, DC, 2 * DC, 2 * DC + c.QC, 2 * DC + c.QC + c.KVC, 2 * DC + c.QC + c.KVC + 1
            nblocks = b_gg + 2 * DC
            blocks = []
            XLd = XLp if kind == "ctx" else XL
            for b in range(nblocks):
                if b < b_gy:
                    if do_main or kind == "ctx":
                        blocks.append(b)
                elif b < b_cq:
                    if do_main:
                        blocks.append(b)
                elif b < b_ckv:
                    if do_q:
                        blocks.append(b)
                elif b < b_gg:
                    if do_kv:
                        blocks.append(b)
                elif do_main:
                    blocks.append(b)
            cur_tile = [None, None, None]
            pend = {}
            KSKIP = os.environ.get("KSKIP", "").split(",")
            if "xl" in KSKIP:
                blocks = [b for b in blocks if not b < b_gy]
            if "gy" in KSKIP:
                blocks = [b for b in blocks if not (b_gy <= b < b_cq)]
            if "cq" in KSKIP:
                blocks = [b for b in blocks if not (b_cq <= b < b_ckv)]
            if "gg" in KSKIP:
                blocks = [b for b in blocks if not (b >= b_gg)]

            def flush(key):
                if key not in pend:
                    return
                si, cnt, b0 = pend.pop(key)
                if key == "xl":
                    src = r3(stg[si], 4)[:, 0:cnt, :]
                    dst = XLd[b0:b0 + cnt, :, xlcol:xlcol + NT].rearrange("m p t -> p m t")
                    store(dst, src, [stgT[si]], [TXLp if kind == "ctx" else TXL])
                elif key == "gy":
                    src = r3(stg[si].bitcast(BF16), 8)[:, 0:cnt, :]
                    dst = GY[b0 - b_gy:b0 - b_gy + cnt, :, tcol:tcol + NT].rearrange("m p t -> p m t")
                    store(dst, src, [stgT[si]], [TGY])
                elif key == "gg":
                    src = r3(stg[si].bitcast(BF16), 8)[:, 0:cnt, :]
                    dst = GG[b0 - b_gg:b0 - b_gg + cnt, :, tcol:tcol + NT].rearrange("m p t -> p m t")
                    store(dst, src, [stgT[si]], [TGG])

            def stage(key, b, cap):
                if key in pend and pend[key][1] == cap:
                    flush(key)
                if key not in pend:
                    pend[key] = [get_stg(), 0, b]
                ent = pend[key]
                pos = ent[1]
                ent[1] += 1
                return ent[0], pos

            for bi, b in enumerate(blocks):
                if prep_next is not None and bi == min(len(blocks) - 1, 24):
                    prep_next()
                pass
                tix = b // nblk_tile
                if cur_tile[0] != tix:
                    wt, wv = wload("in", 0, tix)
                    cur_tile[0], cur_tile[1], cur_tile[2] = tix, wt, wv
                wt, wv = cur_tile[1], cur_tile[2]
                co = (b % nblk_tile) * 128
                is_kr = (b == b_kr)
                if not is_kr:
                    bank = next_pm()
                    P.add("pe", mm_group(lambda k, wv=wv, co=co: wv[:, k, co:co + 128], lambda k: nT[slot][:, k, :], DC, bank),
                          reads=[wt, nTT[slot]], writes=[PS[bank]])
                if b < b_gy:
                    si, pos = stage("xl", b, 4)
                    P.add("act", (lambda e, si=si, pos=pos, bank=bank: e.activation(out=r3(stg[si], 4)[:, pos, :], in_=psb[bank], func=AF.Copy)),
                          reads=[PS[bank]], writes=[stgT[si]])
                elif b < b_cq:
                    si, pos = stage("gy", b, 8)
                    P.add("act", (lambda e, si=si, pos=pos, bank=bank: e.activation(out=r3(stg[si].bitcast(BF16), 8)[:, pos, :], in_=psb[bank], func=AF.Gelu_apprx_tanh)),
                          reads=[PS[bank]], writes=[stgT[si]])
                elif b < b_ckv:
                    j = b - b_cq
                    P.add("act", (lambda e, bank=bank: e.activation(out=sq, in_=psb[bank], func=AF.Square)),
                          reads=[PS[bank]], writes=[sqT])
                    P.add("dve", (lambda e, j=j, bank=bank: e.tensor_scalar(out=cq[:, j, :], in0=psb[bank], scalar1=pv_qn[:, j:j + 1], scalar2=None, op0=ALU.mult)),
                          reads=[PS[bank], tCs], writes=[cqT])
                    P.add("pe", (lambda e: e.matmul(psb[7], lhsT=ones, rhs=sq, start=True, stop=True)),
                          reads=[sqT, tO], writes=[PS[7]])
                    if j == 0:
                        P.add("dve", (lambda e: e.tensor_copy(out=rstdq, in_=psb[7])), reads=[PS[7]], writes=[rqT])
                    else:
                        P.add("dve", (lambda e: e.tensor_tensor(out=rstdq, in0=psb[7], in1=rstdq, op=ALU.add)), reads=[PS[7], rqT], writes=[rqT])
                    if j == c.QC - 1 and "rstd" not in KSKIP:
                        P.add("act", (lambda e: e.activation(out=rstdq, in_=rstdq, func=AF.Sqrt, scale=1.0 / c.QL, bias=EPS)),
                              reads=[rqT], writes=[rqT])
                        P.add("dve", (lambda e: e.reciprocal(out=rstdq, in_=rstdq)), reads=[rqT], writes=[rqT])
                        if "poolq" not in KSKIP:
                          P.add("pool", (lambda e: e.tensor_scalar(out=rstdq, in0=rstdq, scalar1=SCALE, scalar2=None, op0=ALU.mult)),
                              reads=[rqT], writes=[rqT])
                        if "poolq2" not in KSKIP:
                          P.add("pool", (lambda e: e.tensor_tensor(out=cosq, in0=cos2, in1=rstdq[0:64, :], op=ALU.mult)),
                              reads=[rqT, ropeT], writes=[ropeqT])
                          P.add("pool", (lambda e: e.tensor_tensor(out=sinq, in0=sins, in1=rstdq[0:64, :], op=ALU.mult)),
                              reads=[rqT, ropeT], writes=[ropeqT])
                elif b < b_kr:
                    j = b - b_ckv
                    P.add("act", (lambda e, bank=bank: e.activation(out=sq, in_=psb[bank], func=AF.Square)),
                          reads=[PS[bank]], writes=[sqT])
                    P.add("dve", (lambda e, j=j, bank=bank: e.tensor_scalar(out=ckv[:, j, :], in0=psb[bank], scalar1=pv_kvn[:, j:j + 1], scalar2=None, op0=ALU.mult)),
                          reads=[PS[bank], tCs], writes=[ckvT])
                    P.add("pe", (lambda e: e.matmul(psb[7], lhsT=ones, rhs=sq, start=True, stop=True)),
                          reads=[sqT, tO], writes=[PS[7]])
                    if j == 0:
                        P.add("dve", (lambda e: e.tensor_copy(out=rstdkv, in_=psb[7])), reads=[PS[7]], writes=[rkT])
                    else:
                        P.add("dve", (lambda e: e.tensor_tensor(out=rstdkv, in0=psb[7], in1=rstdkv, op=ALU.add)), reads=[PS[7], rkT], writes=[rkT])
                    def tokss(e, j=j):
                        ins = None
                        for s_ in range(4):
                            ins = e.matmul(psb[6][:, s_:s_ + 1], lhsT=sq[:, s_ * 128:(s_ + 1) * 128], rhs=ones[:, 0:1], start=True, stop=True)
                        return ins
                    P.add("pe", tokss, reads=[sqT, tO], writes=[PS[6]])
                    if j == 0:
                        P.add("dve", (lambda e: e.tensor_copy(out=rstdtok, in_=psb[6][:, 0:4])), reads=[PS[6]], writes=[rtT])
                    else:
                        P.add("dve", (lambda e: e.tensor_tensor(out=rstdtok, in0=psb[6][:, 0:4], in1=rstdtok, op=ALU.add)), reads=[PS[6], rtT], writes=[rtT])
                    if j == c.KVC - 1:
                        P.add("act", (lambda e: e.activation(out=rstdkv, in_=rstdkv, func=AF.Sqrt, scale=1.0 / c.KVL, bias=EPS)),
                              reads=[rkT], writes=[rkT])
                        P.add("dve", (lambda e: e.reciprocal(out=rstdkv, in_=rstdkv)), reads=[rkT], writes=[rkT])
                        P.add("act", (lambda e: e.activation(out=rstdtok, in_=rstdtok, func=AF.Sqrt, scale=1.0 / c.KVL, bias=EPS)),
                              reads=[rtT], writes=[rtT])
                        P.add("dve", (lambda e: e.reciprocal(out=rstdtok, in_=rstdtok)), reads=[rtT], writes=[rtT])
                elif is_kr:
                    ba_, bb_ = next_pm(), next_pm()
                    for bank, c0 in ((ba_, co), (bb_, co + 64)):
                        P.add("pe", mm_group(lambda k, wv=wv, c0=c0: wv[:, k, c0:c0 + 64], lambda k: nT[slot][:, k, :], DC, bank, M=64),
                              reads=[wt, nTT[slot]], writes=[PS[bank]])
                    ti = 0
                    P.add("dve", (lambda e, ba_=ba_, ti=ti: e.tensor_tensor(out=t1[ti][0:64, :], in0=psb[ba_][0:64, :], in1=cos2, op=ALU.mult)),
                          reads=[PS[ba_], ropeT], writes=[t12T[ti]])
                    P.add("dve", (lambda e, bb_=bb_, ti=ti: e.tensor_tensor(out=t2[ti][0:64, :], in0=psb[bb_][0:64, :], in1=sins, op=ALU.mult)),
                          reads=[PS[bb_], ropeT], writes=[t12T[ti]])
                    si = get_stg()
                    P.add("pool", (lambda e, si=si, ti=ti: e.tensor_tensor(out=stg[si].bitcast(BF16)[0:64, 0:NT], in0=t1[ti][0:64, :], in1=t2[ti][0:64, :], op=ALU.add)),
                          reads=[t12T[ti]], writes=[stgT[si]])
                    store(kvdst[1][:, kvcol:kvcol + NT], stg[si].bitcast(BF16)[0:64, 0:NT], [stgT[si]], [kvdst[4]])
                else:
                    si, pos = stage("gg", b, 8)
                    P.add("act", (lambda e, si=si, pos=pos, bank=bank: e.activation(out=r3(stg[si].bitcast(BF16), 8)[:, pos, :], in_=psb[bank], func=AF.Sigmoid)),
                          reads=[PS[bank]], writes=[stgT[si]])
                if b == b_gy - 1 or (kind == "ctx" and b == b_gy - 1):
                    flush("xl")
                if b == b_cq - 1:
                    flush("gy")
            flush("gg")

            if dq:
                sq_ = specs["qup"]
                hp = sq_.MWt // 256
                for h in range(c.MH):
                    if h % hp == 0:
                        wt, wv = wload("qup", 0, h // hp)
                    co = (h % hp) * 256
                    bank = next_pm()
                    P.add("pe", mm_group(lambda k, wv=wv, co=co: wv[:, k, co:co + 128], lambda k: cq[:, k, :], c.QC, bank),
                          reads=[wt, cqT], writes=[PS[bank]])
                    if h % 8 == 0:
                        sn = get_stg()
                    pos = h % 8
                    P.add("dve", (lambda e, sn=sn, pos=pos, bank=bank: e.tensor_tensor(out=r3(stg[sn].bitcast(BF16), 8)[:, pos, :], in0=psb[bank], in1=rstdq, op=ALU.mult)),
                          reads=[PS[bank], rqT], writes=[stgT[sn]])
                    ba_, bb_ = next_pm(), next_pm()
                    for bank2, c0 in ((ba_, co + 128), (bb_, co + 192)):
                        P.add("pe", mm_group(lambda k, wv=wv, c0=c0: wv[:, k, c0:c0 + 64], lambda k: cq[:, k, :], c.QC, bank2, M=64),
                              reads=[wt, cqT], writes=[PS[bank2]])
                    ti = h % 2
                    P.add("dve", (lambda e, ba_=ba_, ti=ti: e.tensor_tensor(out=t1[ti][0:64, :], in0=psb[ba_][0:64, :], in1=cosq, op=ALU.mult)),
                          reads=[PS[ba_], ropeqT], writes=[t12T[ti]])
                    P.add("dve", (lambda e, bb_=bb_, ti=ti: e.tensor_tensor(out=t2[ti][0:64, :], in0=psb[bb_][0:64, :], in1=sinq, op=ALU.mult)),
                          reads=[PS[bb_], ropeqT], writes=[t12T[ti]])
                    if h % 8 == 0:
                        sr = get_stg()
                    P.add("pool", (lambda e, sr=sr, pos=pos, ti=ti: e.tensor_tensor(out=r3(stg[sr].bitcast(BF16), 8)[0:64, pos, :], in0=t1[ti][0:64, :], in1=t2[ti][0:64, :], op=ALU.add)),
                          reads=[t12T[ti]], writes=[stgT[sr]])
                    if h % 8 == 7 or h == c.MH - 1:
                        h0 = h - pos
                        cnt = pos + 1
                        store(QN[h0:h0 + cnt, :, tcol:tcol + NT].rearrange("m p t -> p m t"), r3(stg[sn].bitcast(BF16), 8)[:, 0:cnt, :], [stgT[sn]], [TQN])
                        store(QR[h0:h0 + cnt, :, tcol:tcol + NT].rearrange("m p t -> p m t"), r3(stg[sr].bitcast(BF16), 8)[0:64, 0:cnt, :], [stgT[sr]], [TQR])

            if do_kv:
                KNd, KRd, Vd, tKN, tKR, tV = kvdst
                sk = specs["kvk"]
                hp = sk.MWt // 128
                for h in range(c.MH):
                    if h % hp == 0:
                        wt, wv = wload("kvk", 0, h // hp)
                    co = (h % hp) * 128
                    bank = next_pm()
                    P.add("pe", mm_group(lambda k, wv=wv, co=co: wv[:, k, co:co + 128], lambda k: ckv[:, k, :], c.KVC, bank),
                          reads=[wt, ckvT], writes=[PS[bank]])
                    if h % 8 == 0:
                        sn = get_stg()
                    pos = h % 8
                    P.add("dve", (lambda e, sn=sn, pos=pos, bank=bank: e.tensor_tensor(out=r3(stg[sn].bitcast(BF16), 8)[:, pos, :], in0=psb[bank], in1=rstdkv, op=ALU.mult)),
                          reads=[PS[bank], rkT], writes=[stgT[sn]])
                    if h % 8 == 7 or h == c.MH - 1:
                        h0 = h - pos
                        cnt = pos + 1
                        store(KNd[h0:h0 + cnt, :, kvcol:kvcol + NT].rearrange("m p t -> p m t"), r3(stg[sn].bitcast(BF16), 8)[:, 0:cnt, :], [stgT[sn]], [tKN])
                sv = specs["kvv"]
                for ng in range(sv.NG):
                    wt, wv = wload("kvv", 0, ng)
                    for n0 in range(0, sv.MWt, 512):
                        sn = get_stg()
                        for s_ in range(4):
                            bank = next_pm()
                            P.add("pe", mm_group(lambda k, s_=s_: ckv[:, k, s_ * 128:(s_ + 1) * 128], lambda k, wv=wv, n0=n0: wv[:, k, n0:n0 + 512], c.KVC, bank),
                                  reads=[wt, ckvT], writes=[PS[bank]])
                            P.add("act", (lambda e, sn=sn, s_=s_, bank=bank: e.activation(out=r3(stg[sn].bitcast(BF16), 8)[:, s_, :], in_=psb[bank], func=AF.Copy, scale=rstdtok[:, s_:s_ + 1])),
                                  reads=[PS[bank], rtT], writes=[stgT[sn]])
                        cols = ng * sv.MWt + n0
                        store(Vd[kvcol:kvcol + NT, cols:cols + 512].rearrange("(s p) n -> p s n", p=128), r3(stg[sn].bitcast(BF16), 8)[:, 0:4, :], [stgT[sn]], [tV])

        sched = []
        for i in range(SP // NT):
            sched.append(dict(kind="ctx", src=xp, row0=i * NT, tcol=None, xlcol=2 + i * NT, pos0=i * NT, posoff=None, kv="p", kvcol=i * NT))
        for i in range(TP // NT):
            sched.append(dict(kind="prompt", src=xown, row0=i * NT, tcol=i * NT, xlcol=2 + i * NT, pos0=i * NT, posoff=1, kv=None, kvcol=None))
        for i in range(SS // NT):
            sched.append(dict(kind="sample", src=xown, row0=TP + i * NT, tcol=TP + i * NT, xlcol=TP + 6 + i * NT, pos0=i * NT, posoff=None, kv="s", kvcol=i * NT))

        def kvd(k):
            if k is None:
                return None
            if k == "p":
                return (KNp, KRp, Vp, TK["p"][0], TK["p"][1], TK["p"][2])
            return (KNs, KRs, Vs, TK["s"][0], TK["s"][1], TK["s"][2])

        import os
        if os.environ.get("KSCHED"):
            sched = [x for x in sched if x["kind"] in os.environ["KSCHED"].split(",")]
        if stop == "p0":
            sched = []
        elif stop is not None and stop.startswith("t"):
            sched = sched[:int(stop[1:])]
        if sched:
            prep_tile(sched[0]["src"], sched[0]["row0"], 0, 0)
        if stop == "prep":
            sched = []
        for i, sc in enumerate(sched):
            nxt = None
            if i + 1 < len(sched):
                n_ = sched[i + 1]
                nxt = (lambda n_=n_, i=i: prep_tile(n_["src"], n_["row0"], (i + 1) % 2, i + 1))
            in_proj_tile(i % 2, sc["kind"], sc["tcol"], sc["xlcol"], sc["pos0"], sc["posoff"], kvd(sc["kv"]), sc["kvcol"], prep_next=nxt)

        hx = xbuf[0][0:4, :]
        hn = nbuf[0:4, :]
        hT = r3(sb.bf16(DC * 4), DC)
        hxT, hnT, hTT = xT[0], nTt, T("hT")
        P.add("sp", lambda e: e.dma_start(out=hx, in_=xhalo[:, :]), writes=[hxT], dma=True)
        P.add("act", lambda e: e.activation(out=hn, in_=hx, func=AF.Square, accum_out=small[0:4, 0:1]), reads=[hxT], writes=[hnT, smallT[0]])
        P.add("act", lambda e: e.activation(out=small[0:4, 1:2], in_=small[0:4, 0:1], func=AF.Sqrt, scale=1.0 / D, bias=EPS), reads=[smallT[0]], writes=[smallT[1]])
        P.add("dve", lambda e: e.reciprocal(out=small[0:4, 1:2], in_=small[0:4, 1:2]), reads=[smallT[1]], writes=[smallT[1]])
        P.add("act", lambda e: e.activation(out=hn, in_=hx, func=AF.Copy, scale=small[0:4, 1:2]), reads=[hxT, smallT[1]], writes=[hnT])
        pbf0 = psb[0].bitcast(BF16)
        for g in range(0, DC, 8):
            ng = min(8, DC - g)

            def trh(e, g=g, ng=ng):
                ins = None
                for j in range(ng):
                    ins = e.transpose(pbf0[:, j * 4:(j + 1) * 4], hn[:, (g + j) * 128:(g + j + 1) * 128], ident[0:4, 0:4])
                return ins
            P.add("pe", trh, reads=[hnT, tC], writes=[PS[0]])
            P.add("dve", (lambda e, g=g, ng=ng: e.tensor_tensor(out=hT[:, g:g + ng, :], in0=r3(pbf0[:, 0:ng * 4], ng),
                                                                 in1=pv_norm1[:, g:g + ng].unsqueeze(2).to_broadcast([128, ng, 4]), op=ALU.mult)),
                  reads=[PS[0], tCs], writes=[hTT])
        s_in = specs["in"]
        nblk_tile = s_in.MWt // 128
        hst = sb.f32(4 * DC)
        hstT = T("hst")
        for b in range(DC):
            if b % nblk_tile == 0:
                wt, wv = wload("in", 0, b // nblk_tile)
            co = (b % nblk_tile) * 128
            bank = next_pm()
            P.add("pe", mm_group(lambda k, wv=wv, co=co: wv[:, k, co:co + 128], lambda k: hT[:, k, :], DC, bank, N=4),
                  reads=[wt, hTT], writes=[PS[bank]])
            P.add("act", (lambda e, b=b, bank=bank: e.activation(out=hst[:, b * 4:(b + 1) * 4], in_=psb[bank][:, 0:4], func=AF.Copy)),
                  reads=[PS[bank]], writes=[hstT])
        hst3 = r3(hst, DC)
        store(XL[:, :, 0:2].rearrange("m p t -> p m t"), hst3[:, :, 0:2], [hstT], [TXL], slow=True)
        store(XL[:, :, TP + 2:TP + 3].rearrange("m p t -> p m t"), hst3[:, :, 2:3], [hstT], [TXL], slow=True)

        P.barrier()
        sb.release(m1)

        if stop != "s1":
            m2 = sb.mark()
            set_wslots(specs["ga"].cols)
            WM = max(TP, SS)
            x2_ = r3(sb.f32(2 * (WM + 8)), 2)
            xc_ = r3(sb.f32(2 * WM), 2)
            xcb_ = r3(sb.bf16(2 * WM), 2)
            rr = sb.f32(4 * WM)
            ii = sb.f32(4 * WM)
            tmp = sb.f32(2 * WM)
            HF_ = sb.f32(WM)
            HB_ = sb.f32(WM)
            zer = sb.f32(2 * DC)
            gyb = r3(sb.bf16(2 * WM), 2)
            hyb = r3(sb.bf16(2 * WM), 2)
            cneg = sb.f32(2 * DC)
            c2 = sb.f32(2 * DC)
            ccin = sb.f32(4 * DC)
            ccall = sb.f32(NCORES * 4 * DC)
            car = sb.f32(2 * DC)
            chn = sb.f32(DC)
            zT, gyT, hyT = T("zer"), T("gyb"), T("hyb")
            x2T_, xcT_, xcbT_, HFT_, HBT_ = [[T(n + str(p)) for p in range(2)] for n in ("x2", "xc", "xcb", "HF", "HB")]
            rrT_ = [[T("rr%d_%d" % (p, i)) for i in range(4)] for p in range(2)]
            iiT_ = [[T("ii%d_%d" % (p, i)) for i in range(4)] for p in range(2)]
            tmpT_ = [T("tmp%d" % i) for i in range(4)]
            item = [0]
            cT, ccT, carT, chnT = T("cneg", small=True), T("ccin", small=True), T("car", small=True), T("chn", small=True)
            racc = sb.f32(64)
            raccT_ = [[T("racc%d_%d" % (p, i), small=True) for i in range(4)] for p in range(2)]
            THS, TPF, TPB, THY = T("HS"), T("PF"), T("PB"), T("HY")
            TCC = T("CC")
            for d in range(2):
                P.add("act", (lambda e, d=d: e.activation(out=cneg[:, d * DC:(d + 1) * DC], in_=pv_lam[d], func=AF.Exp, scale=-1.0)), reads=[tCs], writes=[cT])
            P.add("act", lambda e: e.activation(out=cneg, in_=cneg, func=AF.Ln, scale=1.0, bias=1.0), reads=[cT], writes=[cT])
            P.add("dve", lambda e: e.tensor_scalar(out=c2, in0=cneg, scalar1=-16.0, scalar2=None, op0=ALU.mult), reads=[cT], writes=[cT])
            P.add("dve", lambda e: e.tensor_scalar(out=cneg, in0=cneg, scalar1=-8.0, scalar2=None, op0=ALU.mult), reads=[cT], writes=[cT])
            P.add("pool", lambda e: e.memset(zer, 0.0), writes=[zT])
            sga = specs["ga"]
            hpt = sga.MWt // 1024 if sga.MWt >= 1024 else 1

            def ga_cols(d, g, h):
                return ((d * 2 + g) * c.LH + h) * 256

            def lru_region(h, mode, r0, W, t0, jc=None):
                ch = [2 * h, 2 * h + 1]
                dirs = [d for d in range(2) if not (mode == "ctx" and ((d == 0 and jc == NCORES - 1) or (d == 1 and jc == 0)))]
                half = (W <= WM // 2)
                par = (item[0] % 2) if half else 0
                item[0] += 1
                wo = par * (WM // 2)
                x2 = x2_[:, :, par * (WM // 2 + 4):par * (WM // 2 + 4) + W + 4] if half else x2_
                xc, xcb, HF, HB = xc_[:, :, wo:wo + W], xcb_[:, :, wo:wo + W], HF_[:, wo:wo + W], HB_[:, wo:wo + W]
                x2T, xcT, xcbT, HFT, HBT = x2T_[par], xcT_[par], xcbT_[par], HFT_[par], HBT_[par]
                rrT, iiT = rrT_[par], iiT_[par]
                raccT, ro = raccT_[par], par * 32
                src = XLp if mode == "ctx" else XL
                tsrc = TXLp if mode == "ctx" else TXL
                if mode == "sample":
                    P.add("sp", (lambda e: e.dma_start(out=x2[:, :, 2:W + 2], in_=src[ch[0]:ch[0] + 2, :, r0 + 2:r0 + W + 2].rearrange("m p t -> p m t"))),
                          reads=[tsrc], writes=[x2T], dma=True)
                    P.add("pool", lambda e: e.memset(x2[:, :, 0:2], 0.0), writes=[x2T])
                    P.add("pool", lambda e: e.memset(x2[:, :, W + 2:W + 4], 0.0), writes=[x2T])
                else:
                    P.add("sp", (lambda e: e.dma_start(out=x2[:, :, 0:W + 3], in_=src[ch[0]:ch[0] + 2, :, r0:r0 + W + 3].rearrange("m p t -> p m t"))),
                          reads=[tsrc], writes=[x2T], dma=True)
                if mode != "ctx":
                    P.add("sp", (lambda e: e.dma_start(out=gyb[:, :, 0:W], in_=GY[ch[0]:ch[0] + 2, :, t0:t0 + W].rearrange("m p t -> p m t"))),
                          reads=[TGY], writes=[gyT], dma=True)
                for oc in range(2):
                    cc_ = ch[oc]
                    P.add("act", (lambda e, oc=oc, cc_=cc_: e.activation(out=xc[:, oc, 0:W], in_=x2[:, oc, 0:W], func=AF.Identity,
                                                                          scale=pv_convw[0][:, cc_:cc_ + 1], bias=pv_convb[:, cc_:cc_ + 1])),
                          reads=[x2T, tCs], writes=[xcT])
                    for k in range(1, 4):
                        P.add("dve", (lambda e, oc=oc, cc_=cc_, k=k: e.scalar_tensor_tensor(out=xc[:, oc, 0:W], in0=x2[:, oc, k:k + W], scalar=pv_convw[k][:, cc_:cc_ + 1],
                                                                                           in1=xc[:, oc, 0:W], op0=ALU.mult, op1=ALU.add)),
                              reads=[x2T, xcT, tCs], writes=[xcT])
                P.add("pool", lambda e: e.tensor_copy(out=xcb[:, :, 0:W], in_=xc[:, :, 0:W]), reads=[xcT], writes=[xcbT])
                for d in dirs:
                    for g in range(2):
                        col = ga_cols(d, g, h)
                        tix, cin = col // sga.MWt, col % sga.MWt
                        wt, wv = wload("ga", 0, tix)
                        for oc in range(2):
                            cc_ = ch[oc]
                            dst = (rr if g == 0 else ii)
                            dT = (rrT if g == 0 else iiT)[d * 2 + oc]
                            bvec = (pv_ba if g == 0 else pv_bx)[d][:, cc_:cc_ + 1]
                            for tt in range(W // NT):
                                bank = next_pm()
                                P.add("pe", mm_group(lambda k, wv=wv, cin=cin, oc=oc: wv[:, k, cin + oc * 128:cin + (oc + 1) * 128],
                                                     lambda k, tt=tt: xcb[:, k, tt * NT:(tt + 1) * NT], 2, bank),
                                      reads=[wt, xcbT], writes=[PS[bank]])
                                o_ = (d * 2 + oc) * WM + wo + tt * NT
                                if mode == "ctx" and g == 0:
                                    ai = ro + (d * 2 + oc) * 8 + tt
                                    P.add("act", (lambda e, dst=dst, o_=o_, bank=bank, bvec=bvec, ai=ai: e.activation(out=dst[:, o_:o_ + NT], in_=psb[bank], func=AF.Sigmoid, bias=bvec,
                                                                                                                    accum_out=racc[:, ai:ai + 1])),
                                          reads=[PS[bank], tCs], writes=[dT, raccT[d * 2 + oc]])
                                else:
                                    P.add("act", (lambda e, dst=dst, o_=o_, bank=bank, bvec=bvec: e.activation(out=dst[:, o_:o_ + NT], in_=psb[bank], func=AF.Sigmoid, bias=bvec)),
                                          reads=[PS[bank], tCs], writes=[dT])
                for oc in range(2):
                    cc_ = ch[oc]
                    ent = []
                    for d in dirs:
                        ix = d * 2 + oc
                        rv = rr[:, ix * WM + wo:ix * WM + wo + W]
                        iv = ii[:, ix * WM + wo:ix * WM + wo + W]
                        if half:
                            tsel = par * 2 + d
                            tv = tmp[:, tsel * (WM // 2):tsel * (WM // 2) + W]
                            tvT = tmpT_[tsel]
                        else:
                            tv, tvT = tmp[:, d * WM:d * WM + W], tmpT_[d * 2]
                        ent.append((d, ix, rv, iv, tv, tvT))
                        if mode == "ctx":
                            nacc = W // NT
                            a0 = ro + ix * 8
                            for q_ in range(1, nacc):
                                P.add("dve", (lambda e, a0=a0, q_=q_: e.tensor_tensor(out=racc[:, a0:a0 + 1], in0=racc[:, a0:a0 + 1], in1=racc[:, a0 + q_:a0 + q_ + 1], op=ALU.add)),
                                      reads=[], writes=[raccT[ix]])
                            dstc = (0 if d == 0 else 2 * DC) + cc_
                            P.add("act", (lambda e, a0=a0, d=d, cc_=cc_, dstc=dstc: e.activation(out=call3[:, jc, dstc:dstc + 1], in_=racc[:, a0:a0 + 1], func=AF.Exp,
                                                                                                 scale=cneg[:, d * DC + cc_:d * DC + cc_ + 1])),
                                  reads=[raccT[ix], cT], writes=[ccT])
                        P.add("act", (lambda e, rv=rv, d=d, cc_=cc_, tv=tv: e.activation(out=tv, in_=rv, func=AF.Exp, scale=c2[:, d * DC + cc_:d * DC + cc_ + 1])),
                              reads=[rrT[ix], cT], writes=[tvT])
                        P.add("act", (lambda e, rv=rv, d=d, cc_=cc_: e.activation(out=rv, in_=rv, func=AF.Exp, scale=cneg[:, d * DC + cc_:d * DC + cc_ + 1])),
                              reads=[cT], writes=[rrT[ix]])
                    for (d, ix, rv, iv, tv, tvT) in ent:
                        P.add("act", (lambda e, tv=tv: e.activation(out=tv, in_=tv, func=AF.Sqrt, scale=-1.0, bias=1.0)), reads=[], writes=[tvT])
                    for (d, ix, rv, iv, tv, tvT) in ent:
                        P.add("pool", (lambda e, iv=iv, oc=oc: e.tensor_tensor(out=iv, in0=iv, in1=xc[:, oc, 0:W], op=ALU.mult)), reads=[xcT], writes=[iiT[ix]])
                        P.add("dve", (lambda e, iv=iv, tv=tv: e.tensor_tensor(out=iv, in0=iv, in1=tv, op=ALU.mult)), reads=[tvT], writes=[iiT[ix]])
                    af, uf = rr[:, oc * WM + wo:oc * WM + wo + W], ii[:, oc * WM + wo:oc * WM + wo + W]
                    ab, ub = rr[:, (2 + oc) * WM + wo:(2 + oc) * WM + wo + W], ii[:, (2 + oc) * WM + wo:(2 + oc) * WM + wo + W]
                    if mode == "own":
                        inf_, inb_ = car[:, cc_:cc_ + 1], car[:, DC + cc_:DC + cc_ + 1]
                        rd = [carT]
                    else:
                        inf_, inb_, rd = 0.0, 0.0, []
                    if 0 in dirs:
                        P.add("dve", (lambda e, af=af, uf=uf, inf_=inf_: e.tensor_tensor_scan(out=HF[:, 0:W], data0=af, data1=uf, initial=inf_, op0=ALU.mult, op1=ALU.add)),
                              reads=[rrT[oc], iiT[oc]] + rd, writes=[HFT])
                    if 1 in dirs:
                        P.add("dve", (lambda e, ab=ab, ub=ub, inb_=inb_: e.tensor_tensor_scan(out=HB[:, 0:W][:, ::-1], data0=ab[:, ::-1], data1=ub[:, ::-1], initial=inb_, op0=ALU.mult, op1=ALU.add)),
                              reads=[rrT[2 + oc], iiT[2 + oc]] + rd, writes=[HBT])
                    if mode == "ctx":
                        if 0 in dirs:
                            P.add("pool", (lambda e, cc_=cc_: e.tensor_copy(out=call3[:, jc, DC + cc_:DC + cc_ + 1], in_=HF[:, W - 1:W])), reads=[HFT], writes=[ccT])
                        if 1 in dirs:
                            P.add("pool", (lambda e, cc_=cc_: e.tensor_copy(out=call3[:, jc, 3 * DC + cc_:3 * DC + cc_ + 1], in_=HB[:, 0:1])), reads=[HBT], writes=[ccT])
                    else:
                        P.add("pool", (lambda e: e.tensor_tensor(out=HF[:, 0:W], in0=HF[:, 0:W], in1=HB[:, 0:W], op=ALU.add)), reads=[HBT], writes=[HFT])
                        P.add("pool", (lambda e, oc=oc: e.tensor_tensor(out=hyb[:, oc, 0:W], in0=HF[:, 0:W], in1=gyb[:, oc, 0:W], op=ALU.mult)), reads=[HFT, gyT], writes=[hyT])
                if mode != "ctx":
                    store(HY[ch[0]:ch[0] + 2, :, t0:t0 + W].rearrange("m p t -> p m t"), hyb[:, :, 0:W], [hyT], [THY])

            call3 = r3(ccall, NCORES)
            P.add("pool", lambda e: e.memset(ccall, 0.0), writes=[ccT])
            store(XLp[:, :, 0:2].rearrange("m p t -> p m t"), r3(zer[:, 0:2 * DC], DC), [zT], [TXLp], slow=True)
            store(XLp[:, :, SP + 2:SP + 3].rearrange("m p t -> p m t"), r3(zer[:, 0:DC], DC), [zT], [TXLp], slow=True)
            for j in range(NCORES):
                for h in range(c.LH):
                    lru_region(h, "ctx", j * TP, TP, None, jc=j)
                    conv_some(1)
            P.add("dve", lambda e: e.memset(car, 0.0), writes=[carT])
            P.add("dve", lambda e: e.memset(chn, 0.0), writes=[chnT])
            for j in range(NCORES):
                P.add("dve", (lambda e, j=j: e.tensor_tensor(out=chn, in0=chn, in1=call3[:, j, 0:DC], op=ALU.mult)), reads=[ccT], writes=[chnT])
                P.add("dve", (lambda e, j=j: e.tensor_tensor(out=chn, in0=chn, in1=call3[:, j, DC:2 * DC], op=ALU.add)), reads=[ccT], writes=[chnT])
                P.add("dve", (lambda e, j=j: e.scalar_tensor_tensor(out=car[:, 0:DC], in0=chn, scalar=cv[:, 2 + j:3 + j], in1=car[:, 0:DC], op0=ALU.mult, op1=ALU.add)),
                      reads=[chnT, tCs], writes=[carT])
            P.add("dve", lambda e: e.memset(chn, 0.0), writes=[chnT])
            for j in range(NCORES - 1, -1, -1):
                P.add("dve", (lambda e, j=j: e.tensor_tensor(out=chn, in0=chn, in1=call3[:, j, 2 * DC:3 * DC], op=ALU.mult)), reads=[ccT], writes=[chnT])
                P.add("dve", (lambda e, j=j: e.tensor_tensor(out=chn, in0=chn, in1=call3[:, j, 3 * DC:4 * DC], op=ALU.add)), reads=[ccT], writes=[chnT])
                P.add("dve", (lambda e, j=j: e.scalar_tensor_tensor(out=car[:, DC:2 * DC], in0=chn, scalar=cv[:, 10 + j:11 + j], in1=car[:, DC:2 * DC], op0=ALU.mult, op1=ALU.add)),
                      reads=[chnT, tCs], writes=[carT])
            for h in range(c.LH):
                lru_region(h, "own", 0, TP, 0)
                lru_region(h, "sample", TP + 4, SS, TP)
            conv_some()
            P.barrier()
            sb.release(m2)

        TOM = T("OM")
        if stop not in ("s1", "s2"):
            m3 = sb.mark()
            set_wslots(128)
            SM = max(SP, SS)
            QM = max(TP, SS)
            knb = [sb.bf16(SM) for _ in range(2)]
            vb = [r3(sb.bf16(SM), SM // 128) for _ in range(2)]
            krb = sb.bf16(SM, 64)
            qnb = [sb.bf16(QM) for _ in range(2)]
            qrb = [sb.bf16(QM, 64) for _ in range(2)]
            NPT = 4
            ptb = [sb.bf16(NT) for _ in range(NPT)]
            rcp = sb.f32(NT)
            ost = [sb.bf16(NT) for _ in range(2)]
            knT = [T("kn%d" % i) for i in range(2)]
            vT = [T("v%d" % i) for i in range(2)]
            qT = [T("q%d" % i) for i in range(2)]
            krT, rcpT = T("kr"), T("rcp")
            ptT = [T("pt%d" % i) for i in range(NPT)]
            ostT = [T("ost%d" % i) for i in range(2)]
            hctr = 0
            ptc = 0
            qkc = 0
            for (KNd, KRd, Vd, tk, S_, q0, TQ) in ((KNp, KRp, Vp, TK["p"], SP, 0, TP), (KNs, KRs, Vs, TK["s"], SS, TP, SS)):
                NKT = S_ // 128
                P.add("sp", (lambda e, KRd=KRd, S_=S_: e.dma_start(out=krb[:, 0:S_], in_=KRd[:, 0:S_])), reads=[tk[1]], writes=[krT], dma=True)
                for h in range(c.MH):
                    hb_ = hctr % 2
                    hctr += 1
                    P.add("sp", (lambda e, hb_=hb_, KNd=KNd, h=h, S_=S_: e.dma_start(out=knb[hb_][:, 0:S_], in_=KNd[h, :, 0:S_])), reads=[tk[0]], writes=[knT[hb_]], dma=True)
                    P.add("sp", (lambda e, hb_=hb_, Vd=Vd, h=h, S_=S_, NKT=NKT: e.dma_start(out=vb[hb_][:, 0:NKT, :], in_=Vd[0:S_, h * 128:(h + 1) * 128].rearrange("(k p) d -> p k d", p=128))),
                          reads=[tk[2]], writes=[vT[hb_]], dma=True)
                    P.add("sp", (lambda e, hb_=hb_, h=h, q0=q0, TQ=TQ: e.dma_start(out=qnb[hb_][:, 0:TQ], in_=QN[h, :, q0:q0 + TQ])), reads=[TQN], writes=[qT[hb_]], dma=True)
                    P.add("sp", (lambda e, hb_=hb_, h=h, q0=q0, TQ=TQ: e.dma_start(out=qrb[hb_][:, 0:TQ], in_=QR[h, :, q0:q0 + TQ])), reads=[TQR], writes=[qT[hb_]], dma=True)
                    for qt in range(TQ // NT):
                        bo, bd = 3 + (qt % 2), 5 + (qt % 2)
                        qs = slice(qt * NT, (qt + 1) * NT)

                        def qk(kt, bank, hb_=hb_, qs=qs):
                            def f(e):
                                e.matmul(psb[bank], lhsT=knb[hb_][:, kt * 128:(kt + 1) * 128], rhs=qnb[hb_][:, qs], start=True, stop=False)
                                return e.matmul(psb[bank], lhsT=krb[:, kt * 128:(kt + 1) * 128], rhs=qrb[hb_][:, qs], start=False, stop=True)
                            return f
                        banks = {}
                        pts = {}

                        def issue_qk(kt):
                            nonlocal qkc
                            bank = qkc % 3
                            qkc += 1
                            banks[kt] = bank
                            P.add("pe", qk(kt, bank), reads=[knT[hb_], krT, qT[hb_]], writes=[PS[bank]])

                        def issue_exp(kt):
                            nonlocal ptc
                            pi = ptc % NPT
                            ptc += 1
                            pts[kt] = pi
                            bank = banks[kt]
                            P.add("act", (lambda e, pi=pi, bank=bank: e.activation(out=ptb[pi], in_=psb[bank], func=AF.Exp)), reads=[PS[bank]], writes=[ptT[pi]])

                        def issue_pv(kt):
                            pi = pts[kt]
                            last = (kt == NKT - 1)
                            P.add("pe", (lambda e, kt=kt, pi=pi, last=last, bo=bo, hb_=hb_: e.matmul(psb[bo], lhsT=vb[hb_][:, kt, :], rhs=ptb[pi], start=(kt == 0), stop=last)),
                                  reads=[vT[hb_], ptT[pi]], writes=[PS[bo]], noinc=not last)
                            P.add("pe", (lambda e, kt=kt, pi=pi, last=last, bd=bd: e.matmul(psb[bd], lhsT=ones, rhs=ptb[pi], start=(kt == 0), stop=last)),
                                  reads=[tO, ptT[pi]], writes=[PS[bd]], noinc=not last)
                        issue_qk(0)
                        for kt in range(NKT):
                            if kt + 1 < NKT:
                                issue_qk(kt + 1)
                            issue_exp(kt)
                            issue_pv(kt)
                        oi = qt % 2
                        P.add("dve", (lambda e, bd=bd: e.reciprocal(out=rcp, in_=psb[bd])), reads=[PS[bd]], writes=[rcpT])
                        P.add("dve", (lambda e, bo=bo, oi=oi: e.tensor_tensor(out=ost[oi], in0=psb[bo], in1=rcp, op=ALU.mult)), reads=[PS[bo], rcpT], writes=[ostT[oi]])
                        store(OM[h, :, q0 + qt * NT:q0 + (qt + 1) * NT], ost[oi], [ostT[oi]], [TOM])
            P.barrier()
            sb.release(m3)

        THR = T("HR")
        if stop not in ("s1", "s2", "s3"):
            m4 = sb.mark()
            set_wslots(WSLOT_ELEMS)
            hyt = r3(sb.bf16(DC * NT), DC)
            omt = r3(sb.bf16(c.MC * NT), c.MC)
            mgt = r3(sb.bf16(DC * NT), DC)
            ggt = [r3(sb.bf16(2 * NT), 2) for _ in range(2)]
            ta = [sb.f32(NT) for _ in range(2)]
            tb = [sb.f32(NT) for _ in range(2)]
            xr = [r3(sb.f32(4 * NT), 4) for _ in range(2)]
            hytT, omtT, mgtT = T("hyt"), T("omt"), T("mgt")
            ggT = [T("ggt%d" % i) for i in range(2)]
            tabT = [T("tab%d" % i) for i in range(2)]
            xrT = [T("xr%d" % i) for i in range(2)]
            sl, sm, so = specs["lru"], specs["mla"], specs["out"]
            for tt in range(T_ // NT):
                tc0 = tt * NT
                P.add("sp", (lambda e, tc0=tc0: e.dma_start(out=hyt, in_=HY[:, :, tc0:tc0 + NT].rearrange("m p t -> p m t"))), reads=[THY], writes=[hytT], dma=True)
                P.add("sp", (lambda e, tc0=tc0: e.dma_start(out=omt, in_=OM[:, :, tc0:tc0 + NT].rearrange("m p t -> p m t"))), reads=[TOM], writes=[omtT], dma=True)
                nbl = sl.MWt // 128
                nbm = sm.MWt // 128
                for m in range(DC):
                    if m % nbl == 0:
                        wtl, wvl = wload("lru", 0, m // nbl)
                    if m % nbm == 0:
                        wtm, wvm = wload("mla", 0, m // nbm)
                    gi = m % 2
                    P.add("sp", (lambda e, gi=gi, m=m, tc0=tc0: e.dma_start(out=ggt[gi], in_=GG[m:m + DC + 1:DC, :, tc0:tc0 + NT].rearrange("m p t -> p m t"))),
                          reads=[TGG], writes=[ggT[gi]], dma=True)
                    b1, b2 = next_pm(), next_pm()
                    col, com = (m % nbl) * 128, (m % nbm) * 128
                    P.add("pe", mm_group(lambda k, wvl=wvl, col=col: wvl[:, k, col:col + 128], lambda k: hyt[:, k, :], DC, b1), reads=[wtl, hytT], writes=[PS[b1]])
                    P.add("pe", mm_group(lambda k, wvm=wvm, com=com: wvm[:, k, com:com + 128], lambda k: omt[:, k, :], c.MC, b2), reads=[wtm, omtT], writes=[PS[b2]])
                    P.add("dve", (lambda e, gi=gi, b1=b1: e.tensor_tensor(out=ta[gi], in0=psb[b1], in1=ggt[gi][:, 0, :], op=ALU.mult)), reads=[PS[b1], ggT[gi]], writes=[tabT[gi]])
                    P.add("dve", (lambda e, gi=gi, b2=b2: e.tensor_tensor(out=tb[gi], in0=psb[b2], in1=ggt[gi][:, 1, :], op=ALU.mult)), reads=[PS[b2], ggT[gi]], writes=[tabT[gi]])
                    P.add("pool", (lambda e, gi=gi, m=m: e.tensor_tensor(out=mgt[:, m, :], in0=ta[gi], in1=tb[gi], op=ALU.add)), reads=[tabT[gi]], writes=[mgtT])
                for n in range(D // 512):
                    xi = n % 2
                    P.add("sp", (lambda e, xi=xi, n=n, tc0=tc0: e.dma_start(out=xr[xi], in_=xown[tc0:tc0 + NT, n * 512:(n + 1) * 512].rearrange("(s p) n -> p s n", p=128))),
                          writes=[xrT[xi]], dma=True)
                    wts = [wload("out", kg, n) for kg in range(so.KG)]
                    for s_ in range(4):
                        bank = next_pm()

                        def f(e, s_=s_, bank=bank, wts=wts):
                            ins = None
                            for k in range(DC):
                                wv = wts[k // so.KCt][1]
                                ins = e.matmul(psb[bank], lhsT=mgt[:, k, s_ * 128:(s_ + 1) * 128], rhs=wv[:, k % so.KCt, :], start=(k == 0), stop=(k == DC - 1))
                            return ins
                        P.add("pe", f, reads=[mgtT] + [w[0] for w in wts], writes=[PS[bank]])
                        P.add("dve", (lambda e, xi=xi, s_=s_, bank=bank: e.tensor_tensor(out=xr[xi][:, s_, :], in0=psb[bank], in1=xr[xi][:, s_, :], op=ALU.add)),
                              reads=[PS[bank]], writes=[xrT[xi]])
                    store(HR[tc0:tc0 + NT, n * 512:(n + 1) * 512].rearrange("(s p) n -> p s n", p=128), xr[xi], [xrT[xi]], [THR])
            P.barrier()
            sb.release(m4)

            m5 = sb.mark()
            set_wslots(WSLOT_ELEMS)
            hres = r3(sb.f32(4 * D), 4)
            n2b = sb.bf16(D)
            n2T = r3(sb.bf16(DC * NT), DC)
            FP = 8 if c.FC >= 64 else (4 if c.FC >= 4 else 1)
            FCP = c.FC // FP
            uT = r3(sb.bf16(FCP * NT), FCP)
            rl = [sb.f32(NT) for _ in range(2)]
            nfb = sb.f32(D)
            sm2 = sb.f32(16)
            hresT, n2bT, n2TT, uTT, nfT = T("hres"), T("n2b"), T("n2T"), T("uT"), T("nfb")
            rlT = [T("rl%d" % i) for i in range(2)]
            sm2T = [T("sm2_%d" % i, small=True) for i in range(4)]
            TY = T("Y")
            P.add("sp", lambda e: e.dma_start(out=nfb, in_=normf.partition_broadcast(128)), writes=[nfT], dma=True)
            su, sd = specs["up"], specs["down"]
            nbu = su.MWt // 128
            for tt in range(T_ // NT):
                tc0 = tt * NT
                P.add("sp", (lambda e, tc0=tc0: e.dma_start(out=hres, in_=HR[tc0:tc0 + NT, :].rearrange("(s p) n -> p s n", p=128))), reads=[THR], writes=[hresT], dma=True)
                for s_ in range(4):
                    P.add("act", (lambda e, s_=s_: e.activation(out=n2b, in_=hres[:, s_, :], func=AF.Square, accum_out=sm2[:, 0:1])), reads=[hresT], writes=[n2bT, sm2T[0]])
                    P.add("act", (lambda e: e.activation(out=sm2[:, 1:2], in_=sm2[:, 0:1], func=AF.Sqrt, scale=1.0 / D, bias=EPS)), reads=[sm2T[0]], writes=[sm2T[1]])
                    P.add("dve", (lambda e: e.reciprocal(out=sm2[:, 1:2], in_=sm2[:, 1:2])), reads=[sm2T[1]], writes=[sm2T[1]])
                    P.add("act", (lambda e, s_=s_: e.activation(out=n2b, in_=hres[:, s_, :], func=AF.Copy, scale=sm2[:, 1:2])), reads=[hresT, sm2T[1]], writes=[n2bT])
                    for g in range(0, DC, 8):
                        ng = min(8, DC - g)
                        bank = (g // 8) % 2
                        pbf = psb[bank].bitcast(BF16)

                        def tr(e, g=g, ng=ng, pbf=pbf):
                            ins = None
                            for j in range(ng):
                                ins = e.transpose(pbf[:, j * 128:(j + 1) * 128], n2b[:, (g + j) * 128:(g + j + 1) * 128], ident)
                            return ins
                        P.add("pe", tr, reads=[n2bT, tC], writes=[PS[bank]])
                        P.add("dve", (lambda e, g=g, ng=ng, pbf=pbf, s_=s_: e.tensor_tensor(out=n2T[:, g:g + ng, s_ * 128:(s_ + 1) * 128], in0=r3(pbf[:, 0:ng * 128], ng),
                                                                                         in1=pv_norm2[:, g:g + ng].unsqueeze(2).to_broadcast([128, ng, 128]), op=ALU.mult)),
                              reads=[PS[bank], tCs], writes=[n2TT])
                for fp in range(FP):
                    for mb in range(FCP):
                        m = fp * FCP + mb
                        if m % nbu == 0:
                            wt, wv = wload("up", 0, m // nbu)
                        co = (m % nbu) * 128
                        bank = next_pm()
                        P.add("pe", mm_group(lambda k, wv=wv, co=co: wv[:, k, co:co + 128], lambda k: n2T[:, k, :], DC, bank), reads=[wt, n2TT], writes=[PS[bank]])
                        ri = m % 2
                        P.add("act", (lambda e, ri=ri, bank=bank: e.activation(out=rl[ri], in_=psb[bank], func=AF.Relu)), reads=[PS[bank]], writes=[rlT[ri]])
                        P.add("pool", (lambda e, ri=ri, mb=mb: e.tensor_tensor(out=uT[:, mb, :], in0=rl[ri], in1=rl[ri], op=ALU.mult)), reads=[rlT[ri]], writes=[uTT])
                    kg0 = fp * FCP // sd.KCt
                    kg1 = (fp * FCP + FCP - 1) // sd.KCt
                    for n in range(D // 512):
                        wts = [wload("down", kg, n) for kg in range(kg0, kg1 + 1)]
                        for s_ in range(4):
                            bank = next_pm()

                            def f(e, s_=s_, bank=bank, wts=wts, fp=fp, kg0=kg0):
                                ins = None
                                for k in range(FCP):
                                    kgl = fp * FCP + k
                                    wv = wts[kgl // sd.KCt - kg0][1]
                                    ins = e.matmul(psb[bank], lhsT=uT[:, k, s_ * 128:(s_ + 1) * 128], rhs=wv[:, kgl % sd.KCt, :], start=(k == 0), stop=(k == FCP - 1))
                                return ins
                            P.add("pe", f, reads=[uTT] + [w[0] for w in wts], writes=[PS[bank]])
                            P.add("dve", (lambda e, s_=s_, n=n, bank=bank: e.tensor_tensor(out=hres[:, s_, n * 512:(n + 1) * 512], in0=psb[bank], in1=hres[:, s_, n * 512:(n + 1) * 512], op=ALU.add)),
                                  reads=[PS[bank]], writes=[hresT])
                for s_ in range(4):
                    P.add("act", (lambda e, s_=s_: e.activation(out=n2b, in_=hres[:, s_, :], func=AF.Square, accum_out=sm2[:, 2:3])), reads=[hresT], writes=[n2bT, sm2T[2]])
                    P.add("act", (lambda e: e.activation(out=sm2[:, 3:4], in_=sm2[:, 2:3], func=AF.Sqrt, scale=1.0 / D, bias=EPS)), reads=[sm2T[2]], writes=[sm2T[3]])
                    P.add("dve", (lambda e: e.reciprocal(out=sm2[:, 3:4], in_=sm2[:, 3:4])), reads=[sm2T[3]], writes=[sm2T[3]])
                    P.add("dve", (lambda e, s_=s_: e.scalar_tensor_tensor(out=hres[:, s_, :], in0=hres[:, s_, :], scalar=sm2[:, 3:4], in1=nfb, op0=ALU.mult, op1=ALU.mult)),
                          reads=[sm2T[3], nfT], writes=[hresT])
                store(y_out[tc0:tc0 + NT, :].rearrange("(s p) n -> p s n", p=128), hres, [hresT], [TY])
            P.barrier()
            sb.release(m5)

        P.emit()
    return nc


def host_layout(c, inp):
    f = np.float32
    D = c.D
    g = lambda k: np.asarray(inp[k], dtype=f)
    w_in = g("w_in")[0]
    o_q, o_kv, o_kr, o_gg = 2 * D, 2 * D + c.QL, 2 * D + c.QL + c.KVL, 2 * D + c.QL + c.KVL + 64
    kr = w_in[:, o_kr:o_kr + 64]
    pad = c.INC - (w_in.shape[1] + 64)
    w_in_p = np.concatenate([w_in[:, :o_kr], kr, kr[:, 32:], kr[:, :32], w_in[:, o_gg:], np.zeros((D, pad), f)], axis=1)
    wq = g("w_q_up")[0].reshape(c.QL, c.MH, 192)
    wq_p = np.concatenate([wq[:, :, :192], wq[:, :, 160:192], wq[:, :, 128:160]], axis=2).reshape(c.QL, c.MH * 256)
    wkv = g("w_kv_up")[0].reshape(c.KVL, c.MH, 256)
    wkvk = np.ascontiguousarray(wkv[:, :, :128]).reshape(c.KVL, c.MW)
    wkvv = np.ascontiguousarray(wkv[:, :, 128:]).reshape(c.KVL, c.MW)
    wa, wx = g("lru_wa")[0], g("lru_wx")[0]
    ga = np.stack([wa, wx], axis=1)
    ga = ga.transpose(3, 0, 1, 2, 4).reshape(256, 4 * c.LH * 256)
    W = {"in": w_in_p, "qup": wq_p, "kvk": wkvk, "kvv": wkvv, "ga": ga, "lru": g("w_lru_proj")[0], "mla": g("w_mla_proj")[0],
         "out": g("w_out")[0], "up": g("w_up")[0], "down": g("w_down")[0]}
    shared = {}
    for s in weight_specs(c):
        shared["w_" + s.name] = pretile(W[s.name], s.KCt, s.MWt)

    def fm(v):
        v = np.asarray(v, f).reshape(-1, 128)
        return v.T
    cw = g("conv_w")[0]
    ba, bx, lam = g("lru_ba")[0].reshape(2, D), g("lru_bx")[0].reshape(2, D), g("lru_lam")[0]
    cols = [fm(g("norm1")[0]), fm(g("norm2")[0])] + [fm(cw[k]) for k in range(4)] + [fm(g("conv_b")[0])] + \
           [fm(ba[0]), fm(ba[1]), fm(bx[0]), fm(bx[1]), fm(lam[0]), fm(lam[1]), fm(g("q_norm")[0]), fm(g("kv_norm")[0])]
    shared["pvec"] = np.ascontiguousarray(np.concatenate(cols, axis=1))
    shared["normf"] = g("norm_f").reshape(1, D)
    xp = g("x_prompt")[0]
    xs = g("x_sample")
    shared["xp"] = xp
    invf = (1.0 / (np.float32(10000.0) ** (np.arange(0, 64, 2, dtype=f) / np.float32(64)))).astype(f)
    maps = []
    for i in range(NCORES):
        m = dict(shared)
        m["xown"] = np.ascontiguousarray(np.concatenate([xp[i * c.TP:(i + 1) * c.TP], xs[i]], axis=0))
        halo = np.zeros((4, D), f)
        if i > 0:
            halo[0:2] = xp[i * c.TP - 2:i * c.TP]
        if i < NCORES - 1:
            halo[2] = xp[(i + 1) * c.TP]
        m["xhalo"] = halo
        cvv = np.zeros((128, 24), f)
        cvv[0:32, 0] = invf
        cvv[32:64, 0] = invf
        cvv[:, 1] = i * c.TP
        if i > 0:
            cvv[:, 2 + i - 1] = 1.0
        if i < NCORES - 1:
            cvv[:, 10 + i + 1] = 1.0
        m["cvec"] = cvv
        maps.append(m)
    return maps


_CACHE = {}


def kernel(**inputs):
    c = Cfg()
    if "nc" not in _CACHE:
        _CACHE["nc"] = build(c)
    nc = _CACHE["nc"]
    maps = host_layout(c, inputs)
    res = run_bass_kernel_spmd(nc, maps, core_ids=list(range(NCORES))).results
    yp = np.concatenate([res[i]["y"][:c.TP] for i in range(NCORES)], axis=0)[None]
    ys = np.stack([res[i]["y"][c.TP:] for i in range(NCORES)], axis=0)
    return (np.ascontiguousarray(yp, dtype=np.float32), np.ascontiguousarray(ys, dtype=np.float32))
```
